# Optimizing a Trainium2 kernel written in Bass

```python
import math
import jax, jax.numpy as jnp
from jax import lax
import numpy as np

D_MODEL = 1024
BATCH = 8
SEQ = 4096
DEPTH = 1

N_HEADS = 8
HEAD_DIM = 64
ATTN_WIDTH = N_HEADS * HEAD_DIM
MOBA_BLOCK = 256
MOBA_TOPK = 3
Q_BLOCK = 128
POOL_WINDOWS = (2, 4, 8, 16)
N_POOL_GROUPS = 4
POOL_WIDTH = D_MODEL // 2
POOL_GROUP = POOL_WIDTH // N_POOL_GROUPS
N_BUCKETS = 32
MAX_DISTANCE = 128
D_FF = 2816
CONV_WIDTH = 3
N_BRANCHES = 2
ALPHA = (2.0 * DEPTH) ** 0.25
BETA = (8.0 * DEPTH) ** -0.25
LN_EPS = 1e-5
IN_COLS = 3 * ATTN_WIDTH + POOL_WIDTH + N_BRANCHES * D_MODEL
NEG = -1e30

kernel_name = "hybrid_moba_pool_gated_deepnorm"


def layer_norm(x, g, b):
    xf = x.astype(jnp.float32)
    mu = xf.mean(-1, keepdims=True)
    var = jnp.square(xf - mu).mean(-1, keepdims=True)
    return ((xf - mu) * lax.rsqrt(var + LN_EPS) * g.astype(jnp.float32) + b.astype(jnp.float32)).astype(x.dtype)


def rel_bucket(dist):
    max_exact = N_BUCKETS // 2
    n = jnp.maximum(dist, 0)
    nf = jnp.maximum(n, 1).astype(jnp.float32)
    large = max_exact + (jnp.log(nf / max_exact) / math.log(MAX_DISTANCE / max_exact)
                         * (N_BUCKETS - max_exact)).astype(jnp.int32)
    large = jnp.minimum(large, N_BUCKETS - 1)
    return jnp.where(n < max_exact, n, large)


def moba_attention(q, k, v, rel_bias):
    B, S, H, Dh = q.shape
    S_pad = -(-S // MOBA_BLOCK) * MOBA_BLOCK
    pad = ((0, 0), (0, S_pad - S), (0, 0), (0, 0))
    q, k, v = [jnp.pad(a, pad).transpose(0, 2, 1, 3) for a in (q, k, v)]
    NB = S_pad // MOBA_BLOCK
    NC = S_pad // Q_BLOCK
    kb = k.reshape(B, H, NB, MOBA_BLOCK, Dh)
    vb = v.reshape(B, H, NB, MOBA_BLOCK, Dh)
    scale = 1.0 / math.sqrt(Dh)

    kmean = kb.astype(jnp.float32).mean(axis=3)
    gate = jnp.einsum('bhsd,bhnd->bhsn', q.astype(jnp.float32), kmean)
    q_blk = jnp.arange(S_pad) // MOBA_BLOCK
    past = jnp.arange(NB)[None, :] < q_blk[:, None]
    gate = jnp.where(past[None, None], gate, -jnp.inf)
    k_sel = min(MOBA_TOPK, NB)
    _, sel = lax.top_k(gate, k_sel)
    valid = sel < q_blk[None, None, :, None]

    def to_chunks(a):
        tail = a.shape[3:]
        a = a.reshape((B, H, NC, Q_BLOCK) + tail)
        a = jnp.moveaxis(a, 2, 1)
        return a.reshape((B * NC, H, Q_BLOCK) + tail)

    q_c, sel_c, valid_c = to_chunks(q), to_chunks(sel), to_chunks(valid)
    b_ids = jnp.repeat(jnp.arange(B, dtype=jnp.int32), NC)
    c_ids = jnp.tile(jnp.arange(NC, dtype=jnp.int32), B)
    h_idx3 = jnp.arange(H)[:, None, None]
    h_idx4 = jnp.arange(H)[:, None, None, None]
    offs = jnp.arange(MOBA_BLOCK)

    def chunk_fn(args):
        qc, selc, validc, b, c = args
        kb_b = kb[b]
        vb_b = vb[b]
        k_g = kb_b[h_idx3, selc]
        v_g = vb_b[h_idx3, selc]
        own = (c * Q_BLOCK) // MOBA_BLOCK
        k_own = lax.dynamic_index_in_dim(kb_b, own, axis=1, keepdims=False)
        v_own = lax.dynamic_index_in_dim(vb_b, own, axis=1, keepdims=False)
        q_pos = c * Q_BLOCK + jnp.arange(Q_BLOCK)

        s_sel = jnp.einsum('hqd,hqjkd->hqjk', qc, k_g).astype(jnp.float32) * scale
        k_pos_sel = selc[..., None] * MOBA_BLOCK + offs
        bias_sel = rel_bias[h_idx4, rel_bucket(q_pos[None, :, None, None] - k_pos_sel)]
        s_sel = jnp.where(validc[..., None], s_sel + bias_sel.astype(jnp.float32), NEG)

        s_own = jnp.einsum('hqd,hkd->hqk', qc, k_own).astype(jnp.float32) * scale
        d_own = q_pos[:, None] - (own * MOBA_BLOCK + offs)[None, :]
        bias_own = rel_bias[:, rel_bucket(d_own)].astype(jnp.float32)
        s_own = jnp.where((d_own >= 0)[None], s_own + bias_own, NEG)

        logits = jnp.concatenate([s_sel.reshape(H, Q_BLOCK, k_sel * MOBA_BLOCK), s_own], axis=-1)
        p = jax.nn.softmax(logits, axis=-1).astype(v.dtype)
        p_sel = p[..., :k_sel * MOBA_BLOCK].reshape(H, Q_BLOCK, k_sel, MOBA_BLOCK)
        p_own = p[..., k_sel * MOBA_BLOCK:]
        return (jnp.einsum('hqjk,hqjkd->hqd', p_sel, v_g)
                + jnp.einsum('hqk,hkd->hqd', p_own, v_own))

    out = lax.map(chunk_fn, (q_c, sel_c, valid_c, b_ids, c_ids))
    out = out.reshape(B, NC, H, Q_BLOCK, Dh).transpose(0, 1, 3, 2, 4)
    return out.reshape(B, S_pad, H * Dh)[:, :S]


def multiscale_pool(p, w_group, scale):
    B, S, _ = p.shape
    pf = p.astype(jnp.float32)
    cs = jnp.pad(jnp.cumsum(pf, axis=1), ((0, 0), (1, 0), (0, 0)))
    t = jnp.arange(S)
    outs = []
    for g, w in enumerate(POOL_WINDOWS):
        csg = cs[..., g * POOL_GROUP:(g + 1) * POOL_GROUP]
        lag = jnp.pad(csg, ((0, 0), (w - 1, 0), (0, 0)))[:, :S]
        cnt = jnp.minimum(t + 1, w).astype(jnp.float32)
        outs.append((csg[:, 1:] - lag) / cnt[None, :, None])
    pooled = jnp.stack(outs, axis=2)
    diff = (pooled - pf.reshape(B, S, N_POOL_GROUPS, POOL_GROUP)).astype(p.dtype)
    y = jnp.einsum('bsgc,gcd->bsgd', diff, w_group).reshape(B, S, POOL_WIDTH)
    return y * scale


def causal_dwconv(a, w, b):
    S = a.shape[1]
    ap = jnp.pad(a, ((0, 0), (CONV_WIDTH - 1, 0), (0, 0)))
    y = b
    for i in range(CONV_WIDTH):
        y = y + ap[:, i:i + S] * w[i]
    return y


def setup_inputs(seed: int = 0) -> dict:
    key = jax.random.key(seed)
    ks = jax.random.split(key, 20)
    f32 = jnp.float32
    nrm = lambda k, shape, s: jax.random.normal(k, shape, f32) * s
    x = jax.random.normal(ks[0], (BATCH, SEQ, D_MODEL), f32)
    w_in = nrm(ks[1], (DEPTH, D_MODEL, IN_COLS), D_MODEL ** -0.5)
    col_scale = jnp.ones((IN_COLS,), f32).at[2 * ATTN_WIDTH:3 * ATTN_WIDTH].set(BETA)
    w_in = w_in * col_scale
    rel_bias = nrm(ks[2], (N_HEADS, N_BUCKETS), 0.5)
    w_pool_group = nrm(ks[3], (DEPTH, N_POOL_GROUPS, POOL_GROUP, POOL_GROUP), POOL_GROUP ** -0.5)
    pool_scale = 1.0 + nrm(ks[4], (DEPTH, POOL_WIDTH), 0.02)
    w_branch_attn = nrm(ks[5], (DEPTH, ATTN_WIDTH, D_MODEL), ATTN_WIDTH ** -0.5 * BETA)
    w_branch_pool = nrm(ks[6], (DEPTH, POOL_WIDTH, D_MODEL), POOL_WIDTH ** -0.5 * BETA)
    w_out = nrm(ks[7], (DEPTH, D_MODEL, D_MODEL), D_MODEL ** -0.5 * BETA)
    ln1_g = 1.0 + nrm(ks[8], (DEPTH, D_MODEL), 0.02)
    ln1_b = nrm(ks[9], (DEPTH, D_MODEL), 0.02)
    w_ffn_in = nrm(ks[10], (DEPTH, D_MODEL, 2 * D_FF), D_MODEL ** -0.5)
    conv_w = nrm(ks[11], (DEPTH, CONV_WIDTH, D_FF), CONV_WIDTH ** -0.5)
    conv_b = nrm(ks[12], (DEPTH, D_FF), 0.01)
    w_ffn_out = nrm(ks[13], (DEPTH, D_FF, D_MODEL), D_FF ** -0.5 * BETA)
    ln2_g = 1.0 + nrm(ks[14], (DEPTH, D_MODEL), 0.02)
    ln2_b = nrm(ks[15], (DEPTH, D_MODEL), 0.02)
    return {"x": x, "w_in": w_in, "rel_bias": rel_bias, "w_pool_group": w_pool_group,
            "pool_scale": pool_scale, "w_branch_attn": w_branch_attn, "w_branch_pool": w_branch_pool,
            "w_out": w_out, "ln1_g": ln1_g, "ln1_b": ln1_b, "w_ffn_in": w_ffn_in,
            "conv_w": conv_w, "conv_b": conv_b, "w_ffn_out": w_ffn_out,
            "ln2_g": ln2_g, "ln2_b": ln2_b}


def reference(x, w_in, rel_bias, w_pool_group, pool_scale, w_branch_attn, w_branch_pool,
              w_out, ln1_g, ln1_b, w_ffn_in, conv_w, conv_b, w_ffn_out, ln2_g, ln2_b):
    B, S, D = x.shape
    h = x
    for l in range(DEPTH):
        proj = h @ w_in[l]
        q, k, v, p, g = jnp.split(proj, [ATTN_WIDTH, 2 * ATTN_WIDTH, 3 * ATTN_WIDTH,
                                         3 * ATTN_WIDTH + POOL_WIDTH], axis=-1)
        hs = (B, S, N_HEADS, HEAD_DIM)
        y_attn = moba_attention(q.reshape(hs), k.reshape(hs), v.reshape(hs), rel_bias) @ w_branch_attn[l]
        y_pool = multiscale_pool(p, w_pool_group[l], pool_scale[l]) @ w_branch_pool[l]
        gates = jax.nn.sigmoid(g.astype(jnp.float32)).astype(h.dtype).reshape(B, S, N_BRANCHES, D)
        mixed = gates[:, :, 0] * y_attn + gates[:, :, 1] * y_pool
        h = layer_norm(ALPHA * h + mixed @ w_out[l], ln1_g[l], ln1_b[l])
        a, u = jnp.split(h @ w_ffn_in[l], 2, axis=-1)
        a = causal_dwconv(a, conv_w[l], conv_b[l])
        f = (jax.nn.gelu(a, approximate=False) * u) @ w_ffn_out[l]
        h = layer_norm(ALPHA * h + f, ln2_g[l], ln2_b[l])
    return h
```

```python
import math
import os
from contextlib import ExitStack

import numpy as np
import ml_dtypes
import concourse.bass as bass
import concourse.mybir as mybir
from concourse.bass_utils import run_bass_kernel_spmd

F32 = mybir.dt.float32
BF16 = mybir.dt.bfloat16
AF = mybir.ActivationFunctionType
ALU = mybir.AluOpType
AX = mybir.AxisListType

D_MODEL = 1024
SEQ = 4096
N_HEADS = 8
HEAD_DIM = 64
MOBA_BLOCK = 256
POOL_WINDOWS = (2, 4, 8, 16)
N_BUCKETS = 32
MAX_DISTANCE = 128
D_FF = 2816
NFF = D_FF // 128
ALPHA = 2.0 ** 0.25
LN_EPS = 1e-5
NEG = -1e30
T = 512
ENGS = ("pe", "act", "dve", "pool", "sp")
SEM_LIMIT = 30000


class Buf:
    __slots__ = ("name", "w", "r")

    def __init__(self, name):
        self.name = name
        self.w = None
        self.r = {}


class Prog:
    def __init__(self):
        self.ops = {e: [] for e in ENGS}
        self.cnt = {}
        self.waited = {e: {} for e in ENGS}
        self.keys = []
        self.epoch = {e: 0 for e in ENGS}

    def _bump(self, key, n):
        if key not in self.cnt:
            self.cnt[key] = 0
            self.keys.append(key)
        self.cnt[key] += n
        return (key, self.cnt[key])

    def _waits(self, eng, reads, writes):
        need = {}

        def add(ev):
            if ev is None:
                return
            k, v = ev
            if eng == "pe" and k.startswith("E_pe"):
                return
            if need.get(k, 0) < v:
                need[k] = v
        for b in reads:
            add(b.w)
        for b in writes:
            add(b.w)
            for k, v in b.r.items():
                add((k, v))
        out = []
        wd = self.waited[eng]
        for k, v in need.items():
            if wd.get(k, 0) < v:
                wd[k] = v
                out.append((k, v))
        return out

    def _register(self, ev, reads, writes):
        k, v = ev
        for b in reads:
            if b.r.get(k, 0) < v:
                b.r[k] = v
        for b in writes:
            b.w = ev
            b.r = {}

    def op(self, eng, fn, reads=(), writes=()):
        waits = self._waits(eng, reads, writes)
        key = "E_%s_%d" % (eng, self.epoch[eng])
        if self.cnt.get(key, 0) >= SEM_LIMIT:
            self.epoch[eng] += 1
            key = "E_%s_%d" % (eng, self.epoch[eng])
        ev = self._bump(key, 1)
        self.ops[eng].append((fn, waits, ev, 1))
        self._register(ev, reads, writes)
        return ev

    def dma(self, q, fn, reads=(), writes=(), key=None):
        waits = self._waits(q, reads, writes)
        key = key or ("D_" + writes[0].name)
        ev = self._bump(key, 16)
        self.ops[q].append((fn, waits, ev, 16))
        self._register(ev, reads, writes)
        return ev

    def wait_all(self, eng, bufs):
        waits = self._waits(eng, bufs, bufs)
        self.ops[eng].append((None, waits, None, 0))

    def emit(self, nc):
        with ExitStack() as st:
            sems = {}
            for k in self.keys:
                sems[k] = st.enter_context(nc.semaphore(k))
            block = st.enter_context(nc.Block())

            def run(eng_name):
                def body(e):
                    for fn, waits, ev, n in self.ops[eng_name]:
                        for k, v in waits:
                            e.wait_ge(sems[k], v)
                        if fn is not None:
                            ins = fn(e)
                            ins.then_inc(sems[ev[0]], n)
                return body

            block.tensor(run("pe"))
            block.scalar(run("act"))
            block.vector(run("dve"))
            block.gpsimd(run("pool"))
            block.sync(run("sp"))


def _rel_bucket_np(d):
    d = np.asarray(d, dtype=np.int64)
    max_exact = N_BUCKETS // 2
    n = np.maximum(d, 0)
    nf = np.maximum(n, 1).astype(np.float32)
    val = (np.log(nf / np.float32(max_exact)) / np.float32(math.log(MAX_DISTANCE / max_exact))
           * np.float32(N_BUCKETS - max_exact)).astype(np.float32)
    large = max_exact + val.astype(np.int32)
    large = np.minimum(large, N_BUCKETS - 1)
    return np.where(n < max_exact, n, large).astype(np.int64)


def make_consts():
    bf = ml_dtypes.bfloat16
    c = {}
    c["c_ident"] = np.eye(128, dtype=np.float32).astype(bf)
    c["c_identf"] = np.eye(128, dtype=np.float32)
    c["c_jmat"] = np.ascontiguousarray(np.eye(128, dtype=np.float32)[::-1])
    oh = np.zeros((33, 384), np.float32)
    for i in range(383):
        d = i - 127
        if d < 0:
            oh[32, i] = NEG
        else:
            oh[int(_rel_bucket_np(d)), i] += 1.0
            oh[N_BUCKETS - 1, i] -= 1.0
    c["c_oh"] = oh
    m = np.zeros((128, 16, 128), np.float32)
    tp = np.arange(128)[:, None]
    tt = np.arange(128)[None, :]
    for g, w in enumerate(POOL_WINDOWS):
        band = ((tt - tp) >= 0) & ((tt - tp) < w)
        m[:, 0 * 4 + g, :] = band * (1.0 / w) - (tp == tt)
        bandp = (tt + 128 - tp) < w
        m[:, 1 * 4 + g, :] = bandp * (1.0 / w)
        cnt = np.minimum(tt + 1, w).astype(np.float32)
        mf = band * (1.0 / cnt) - (tp == tt)
        hi = mf.astype(bf).astype(np.float32)
        m[:, 2 * 4 + g, :] = hi
        m[:, 3 * 4 + g, :] = mf - hi
    c["c_mtab"] = m.astype(bf)
    nb = SEQ // MOBA_BLOCK
    pm = np.zeros((128, nb, 16), np.float32)
    own = np.zeros((128, nb, 16), np.float32)
    for qb in range(nb):
        for n in range(16):
            pm[:, qb, n] = 0.0 if n < qb else NEG
            own[:, qb, n] = (1.0 if n == qb else 0.0) - 1.0
    c["c_pm"] = pm
    c["c_own"] = own
    kind = np.zeros((16, SEQ), np.float32)
    for n in range(16):
        kind[n, n * MOBA_BLOCK:(n + 1) * MOBA_BLOCK] = 1.0
    c["c_kind"] = kind.astype(bf)
    return c


CONST_SPECS = [("c_ident", [128, 128], BF16), ("c_identf", [128, 128], F32), ("c_jmat", [128, 128], F32),
               ("c_oh", [33, 384], F32), ("c_mtab", [128, 16, 128], BF16), ("c_pm", [128, 16, 16], F32),
               ("c_own", [128, 16, 16], F32), ("c_kind", [16, SEQ], BF16)]

INPUT_SPECS = [("x", [SEQ, D_MODEL]), ("w_in", [D_MODEL, 4096]), ("rel_bias", [8, 32]),
               ("w_pool_group", [4, 128, 128]), ("pool_scale", [4, 128]), ("w_branch_attn", [512, 1024]),
               ("w_branch_pool", [512, 1024]), ("w_out", [1024, 1024]), ("ln1_g", [1, 1024]), ("ln1_b", [1, 1024]),
               ("w_ffn_in", [1024, 2 * D_FF]), ("conv_w", [3 * NFF, 128]), ("conv_b", [NFF, 128]),
               ("w_ffn_out", [D_FF, 1024]), ("ln2_g", [1, 1024]), ("ln2_b", [1, 1024])]


def build(seq=SEQ, dbg=False):
    NCH = seq // T
    nc = bass.Bass("TRN2", target_bir_lowering=False)
    I = {}
    for name, shape in INPUT_SPECS:
        shp = [seq, D_MODEL] if name == "x" else shape
        I[name] = nc.dram_tensor(name, shp, F32, kind="ExternalInput").ap()
    for name, shape, dt in CONST_SPECS:
        I[name] = nc.dram_tensor(name, shape, dt, kind="ExternalInput").ap()
    out = nc.dram_tensor("out", [seq, D_MODEL], F32, kind="ExternalOutput").ap()
    NPIECE = 16 + 4 + 4 + 4 + 22 + 11
    wsc = nc.dram_tensor("wsc", [NPIECE, 128, 2048], BF16, kind="Internal").ap()
    ks_d = nc.dram_tensor("ks_d", [8, 64, seq], BF16, kind="Internal").ap()
    vs_d = nc.dram_tensor("vs_d", [8, 128, seq // 128, 65], BF16, kind="Internal").ap()
    tsc = nc.dram_tensor("tsc", [8, 384], F32, kind="Internal").ap()

    P = Prog()
    st = ExitStack()
    bufs = {}
    STOP = os.environ.get("MK_STOP", "")

    class _Stop(Exception):
        pass

    def stop_here(tag):
        if STOP == tag:
            raise _Stop()

    def B(name):
        if name not in bufs:
            bufs[name] = Buf(name)
        return bufs[name]

    def sb(name, shape, dt):
        return st.enter_context(nc.sbuf_tensor(name, shape, dt))

    RING = 8
    ring = sb("ring", [128, RING, 2048], BF16)
    ident = sb("ident", [128, 128], BF16)
    identf = sb("identf", [128, 128], F32)
    jmat = sb("jmat", [128, 128], F32)
    ohs = sb("ohs", [33, 384], F32)
    rbT = sb("rbT", [33, 8], F32)
    tsb = sb("tsb", [8, 384], F32)
    dth = sb("dth", [128, 16, 128], BF16)
    dtl = sb("dtl", [128, 16, 128], BF16)
    mtab = sb("mtab", [128, 16, 128], BF16)
    pmt = sb("pmt", [128, 16, 16], F32)
    ownt = sb("ownt", [128, 16, 16], F32)
    prow = sb("prow", [92, 128], F32)
    prm = sb("prm", [128, 92], F32)
    lnc = sb("lnc", [128, 2, 1024], F32)
    wpool = sb("wpool", [128, 4, 128], BF16)
    ones64 = sb("ones64", [128, 64], F32)
    kmT = sb("kmT", [64, 8, 16], BF16)
    xst = sb("xst", [128, 1, 1024], F32)
    xbt = sb("xbt", [128, 1, 1024], BF16)
    xT = sb("xT", [128, 8, T], BF16)
    h1T = xT
    qaug_full = sb("qaug", [128, 8, T], BF16)
    qaug = qaug_full[0:80]
    kTs = sb("kTs", [64, 4, T], BF16)
    ksum = sb("ksum", [64, 4, 2], F32)
    vtmp = sb("vtmp", [128, 8, 4, 65], BF16)
    ptm = sb("ptm", [128, 5, 512], BF16)
    diffT = sb("diffT", [128, 4, T], BF16)
    ypgT = sb("ypgT", [128, 4, T], BF16)
    attnT_full = sb("attnT", [128, 8, T], BF16)
    attnT = attnT_full[0:64]
    lr_ = attnT_full[64:65].rearrange("p a b -> p (a b)").bitcast(F32)
    kst = sb("kst", [80, 2, seq], BF16)
    vst = sb("vst", [128, 2, seq // 128, 65], BF16)
    pT = sb("pT", [128, 4, T], BF16)
    stmp = sb("stmp", [128, 2, T], F32)
    t1 = sb("t1", [128, 2, 4, 16], F32)
    m8 = sb("m8", [128, 2, 4, 8], F32)
    thr = sb("thr", [128, 2, 4], F32)
    sel = sb("sel", [128, 2, 4, 16], F32)
    amb = sb("amb", [128, 8, 4, 16], BF16)
    bcs_full = sb("bcs", [128, 2, T], F32)
    bcs = bcs_full[0:64]
    onesb = sb("onesb", [128, 64], BF16)
    sg = sb("sg", [128, 2, T], F32)
    mm = sb("mm", [128, 2, T], F32)
    h1 = sb("h1", [128, 4, 1024], F32)
    h1b = sb("h1b", [128, 1, 1024], BF16)
    stat = sb("stat", [128, 2, 2, 6], F32)
    mv = sb("mv", [128, 2, 2], F32)
    sd = sb("sd", [128, 2], F32)
    rstd = sb("rstd", [128, 2], F32)
    asb = sb("asb", [128, 1, T + 2], F32)
    halo = sb("halo", [128, NFF, 2], F32)
    o1 = sb("o1", [128, 1, T], F32)
    o2 = sb("o2", [128, 1, T], F32)
    gl = sb("gl", [128, 1, T], F32)
    guT = sb("guT", [128, NFF, T], BF16)
    hst = guT[:, 0:8, :].rearrange("p a b -> p (a b)").bitcast(F32).rearrange("p (a b) -> p a b", a=16)
    psum = st.enter_context(nc.psum_tensor("psum", [128, 8, 512], F32))
    mixedT = qaug_full
    QA = [B("qa%d" % h) for h in range(8)]
    if os.environ.get("MK_PROBE"):
        try:
            sb("probe", [128, 60000], F32)
        except AssertionError as ex:
            print("SBUF probe:", str(ex)[:200])

    def PSB(b):
        return psum[:, b, :]

    def PSBF(b):
        return psum[:, b, :].bitcast(BF16)

    psb = [B("ps%d" % i) for i in range(8)]
    rot = {"i": 0}

    def nextbank(lo=0, hi=8):
        b = lo + rot["i"] % (hi - lo)
        rot["i"] += 1
        return b

    def dma(q, o, i, reads, writes, key=None, **kw):
        P.dma(q, lambda e, o=o, i=i, kw=kw: e.dma_start(out=o, in_=i, **kw), reads=reads, writes=writes, key=key)

    def act(o, i, func, reads, writes, **kw):
        P.op("act", lambda e, o=o, i=i, kw=kw: e.activation(out=o, in_=i, func=func, **kw), reads=reads, writes=writes)

    for nm, t_, src in [("ident", ident, "c_ident"), ("identf", identf, "c_identf"), ("jmat", jmat, "c_jmat"),
                        ("ohs", ohs, "c_oh"), ("mtab", mtab, "c_mtab"), ("pmt", pmt, "c_pm"), ("ownt", ownt, "c_own")]:
        dma("sp", t_[:], I[src], [], [B(nm)])
    for b_ in range(2):
        dma("sp", kst[64:80, b_, :], I["c_kind"][:, 0:seq], [], [B("kst%d" % b_)])
    dma("sp", prow[0:66, :], I["conv_w"], [], [B("prow")], key="D_prow")
    dma("sp", prow[66:88, :], I["conv_b"], [], [B("prow")], key="D_prow")
    dma("sp", prow[88:92, :], I["pool_scale"], [], [B("prow")], key="D_prow")
    def load_ln(which):
        for k_, nm in enumerate(["ln%d_g" % which, "ln%d_b" % which]):
            dma("pool", lnc[:, k_, :], I[nm].partition_broadcast(128), [], [B("lnc%d" % k_)])
    dma("sp", rbT[0:32, :], I["rel_bias"].rearrange("h b -> b h"), [], [B("rbT")], allow_slow_non_contiguous=True)
    P.op("dve", lambda e: e.memset(rbT[32:33, :], 1.0), reads=[], writes=[B("rbT1")])
    P.op("dve", lambda e: e.memset(ones64[:], 1.0), writes=[B("ones64")])
    P.op("dve", lambda e: e.memset(onesb[:], 1.0), writes=[B("onesb")])
    P.op("dve", lambda e: e.memset(kmT[:], 0.0), writes=[B("kmT")])
    P.op("dve", lambda e: e.memset(halo[:], 0.0), writes=[B("halo")])
    P.op("dve", lambda e: e.memset(vtmp[:], 1.0), writes=[B("vtmp")])
    P.op("dve", lambda e: e.memset(ptm[:, 0, :], 0.0), writes=[B("ptm0")])
    dma("pool", wpool[:], I["w_pool_group"].rearrange("g c d -> c g d"), [], [B("wpool")])
    P.op("pe", lambda e: e.transpose(out=PSB(7)[:, 0:92], in_=prow[:], identity=identf[0:92, 0:92]),
         reads=[B("prow"), B("identf")], writes=[psb[7]])
    act(prm[:], PSB(7)[:, 0:92], AF.Copy, [psb[7]], [B("prm")])
    P.op("pe", lambda e: e.matmul(PSB(6)[0:8, 0:384], lhsT=rbT[0:33, 0:8], rhs=ohs[0:33, :], start=True, stop=True),
         reads=[B("rbT"), B("rbT1"), B("ohs")], writes=[psb[6]])
    act(tsb[:], PSB(6)[0:8, 0:384], AF.Copy, [psb[6]], [B("tsb")])
    dma("sp", tsc, tsb[:], [B("tsb")], [B("tsc")])
    hank = bass.AP(tensor=tsc.tensor, offset=tsc.offset, ap=[[1, 128], [384, 8], [128, 2], [1, 128]])
    dma("sp", hst[:].rearrange("p (h t) q -> p h t q", t=2), hank, [B("tsc")], [B("guT%d" % i_) for i_ in range(8)])
    for grp in range(4):
        bk = grp
        def f(e, grp=grp, bk=bk):
            ins = None
            for k_ in range(4):
                idx = grp * 4 + k_
                ins = e.matmul(PSB(bk)[:, k_ * 128:(k_ + 1) * 128], lhsT=jmat[:], rhs=hst[:, idx, :], start=True, stop=True)
            return ins
        P.op("pe", f, reads=[B("jmat")] + [B("guT%d" % i_) for i_ in range(8)], writes=[psb[bk]])
        d8 = h1[:, grp, 0:512].rearrange("p (a q) -> p a q", a=4)
        act(d8, PSB(bk).rearrange("p (a q) -> p a q", a=4), AF.Copy, [psb[bk]], [B("h1_%d" % grp)], scale=8.0)
        P.op("dve", lambda e, grp=grp, d8=d8: e.tensor_copy(out=dth[:, grp * 4:(grp + 1) * 4, :], in_=d8), reads=[B("h1_%d" % grp)], writes=[B("dtab")])
        P.op("dve", lambda e, grp=grp, d8=d8: e.tensor_tensor(out=dtl[:, grp * 4:(grp + 1) * 4, :], in0=d8, in1=dth[:, grp * 4:(grp + 1) * 4, :], op=ALU.subtract),
             reads=[B("h1_%d" % grp), B("dtab")], writes=[B("dtab")])

    pieces = []

    def addp(src, npart, shape):
        pieces.append((src, npart, shape))
        return len(pieces) - 1

    w_in_v = I["w_in"].rearrange("(kt p) c -> p kt c", p=128)
    PW = [addp(w_in_v[:, :, c * 256:(c + 1) * 256], 128, (8, 256)) for c in range(16)]
    wba_v = I["w_branch_attn"].rearrange("(hp q) c -> q hp c", q=128)
    PBA = [addp(wba_v[:, :, c * 256:(c + 1) * 256], 128, (4, 256)) for c in range(4)]
    wbp_v = I["w_branch_pool"].rearrange("(g p) c -> p g c", p=128)
    PBP = [addp(wbp_v[:, :, c * 256:(c + 1) * 256], 128, (4, 256)) for c in range(4)]
    wout_v = I["w_out"].rearrange("(kt p) c -> p kt c", p=128)
    PWO = [addp(wout_v[:, :, c * 256:(c + 1) * 256], 128, (8, 256)) for c in range(4)]
    wfi_v = I["w_ffn_in"].rearrange("(kt p) c -> p kt c", p=128)
    PA = [addp(wfi_v[:, :, m_ * 256:(m_ + 1) * 256], 128, (8, 256)) for m_ in range(11)]
    PU = [addp(wfi_v[:, :, D_FF + m_ * 256:D_FF + (m_ + 1) * 256], 128, (8, 256)) for m_ in range(11)]
    wfo_v = I["w_ffn_out"].rearrange("(kt p) c -> p kt c", p=128)
    PFO = [addp(wfo_v[:, m_ * 2:m_ * 2 + 2, :], 128, (2, 1024)) for m_ in range(11)]
    assert len(pieces) == NPIECE
    chunk_seq = [PW[2], PW[3], PW[0], PW[1], PW[4], PW[5], PW[6], PW[7]]
    for dp in range(4):
        chunk_seq += [PW[8 + dp], PW[12 + dp], PBA[dp], PBP[dp]]
    chunk_seq += PWO
    for m_ in range(11):
        chunk_seq += [PA[m_], PU[m_]]
    chunk_seq += PFO + PFO
    full_seq = chunk_seq * NCH
    in_scratch = set()
    wstate = {"next": 0, "cur": 0}
    wsc_buf = [B("wsc_all")] * NPIECE
    ring_buf = [B("ring%d" % i) for i in range(RING)]

    def slot_view(slot, pi):
        src, npart, shape = pieces[pi]
        n = shape[0] * shape[1]
        return ring[0:npart, slot, 0:n].rearrange("p (a b) -> p a b", a=shape[0])

    def issue_load():
        s = wstate["next"]
        if s >= len(full_seq):
            return
        wstate["next"] += 1
        pi = full_seq[s]
        slot = s % RING
        src, npart, shape = pieces[pi]
        n = shape[0] * shape[1]
        if pi in in_scratch:
            dma("sp", ring[0:npart, slot, 0:n], wsc[pi, 0:npart, 0:n], [wsc_buf[pi]], [ring_buf[slot]])
        else:
            dma("pool", slot_view(slot, pi), src, [], [ring_buf[slot]], key="D_ringsw%d" % slot)
            dma("sp", wsc[pi, 0:npart, 0:n], ring[0:npart, slot, 0:n], [ring_buf[slot]], [wsc_buf[pi]])
            in_scratch.add(pi)

    def wget(pi):
        s = wstate["cur"]
        assert full_seq[s] == pi, (s, full_seq[s], pi)
        slot = s % RING
        return slot_view(slot, pi), ring_buf[slot]

    def wdone():
        wstate["cur"] += 1
        issue_load()

    def wtake(pi):
        v, b_ = wget(pi)
        wstate["cur"] += 1
        return v, b_

    for _ in range(RING):
        issue_load()

    cw = lambda f_, i_: prm[:, i_ * NFF + f_: i_ * NFF + f_ + 1]
    cbv = lambda f_: prm[:, 66 + f_: 67 + f_]
    psc = lambda g_: prm[:, 88 + g_: 89 + g_]

    def ln_stats(t, k):
        hb = B("h1_%d" % t)
        for hf in range(2):
            P.op("dve", lambda e, hf=hf: e.bn_stats(out=stat[:, k, hf, :], in_=h1[:, t, hf * 512:(hf + 1) * 512]),
                 reads=[hb], writes=[B("stat%d" % k)])
        P.op("dve", lambda e: e.bn_aggr(out=mv[:, k, :], in_=stat[:, k, :, :].rearrange("p a b -> p (a b)")),
             reads=[B("stat%d" % k)], writes=[B("mv%d" % k)])
        act(sd[:, k:k + 1], mv[:, k, 1:2], AF.Sqrt, [B("mv%d" % k)], [B("sd%d" % k)], bias=LN_EPS, scale=1.0)
        P.op("dve", lambda e: e.reciprocal(out=rstd[:, k:k + 1], in_=sd[:, k:k + 1]), reads=[B("sd%d" % k)], writes=[B("rstd%d" % k)])

    def ln_affine(t, k):
        hb = B("h1_%d" % t)
        src = h1[:, t, :]
        P.op("dve", lambda e: e.scalar_tensor_tensor(out=src, in0=src, scalar=mv[:, k, 0:1], in1=lnc[:, 0, :], op0=ALU.subtract, op1=ALU.mult),
             reads=[hb, B("mv%d" % k), B("lnc0")], writes=[hb])
        P.op("dve", lambda e: e.scalar_tensor_tensor(out=src, in0=src, scalar=rstd[:, k:k + 1], in1=lnc[:, 1, :], op0=ALU.mult, op1=ALU.add),
             reads=[hb, B("rstd%d" % k), B("lnc1")], writes=[hb])

    def layer_norm(t, k, gi, bi):
        ln_stats(t, k)
        ln_affine(t, k)

    def x_dma(j, t):
        r0 = j * T + t * 128
        dma("pool", xst[:, 0, :], I["x"][r0:r0 + 128, :], [], [B("xst0")])

    def x_cast(j, t):
        act(xbt[:, 0, :], xst[:, 0, :], AF.Copy, [B("xst0")], [B("xbt0")])

    FOB = [[4, 5, 6, 7], [0, 1, 2, 3]]

    def x_load(j, t):
        x_dma(j, t)
        x_cast(j, t)

    def x_tr(j, t):
        s2 = 0
        bk = nextbank(0, 2)
        def f(e, s2=s2, bk=bk):
            ins = None
            for kt in range(8):
                ins = e.transpose(out=PSBF(bk)[:, kt * 128:(kt + 1) * 128], in_=xbt[:, s2, kt * 128:(kt + 1) * 128], identity=ident[:])
            return ins
        P.op("pe", f, reads=[B("xbt%d" % s2), B("ident")], writes=[psb[bk]])
        P.op("dve", lambda e, bk=bk, t=t: e.tensor_copy(out=xT[:, :, t * 128:(t + 1) * 128],
                                                       in_=PSBF(bk).rearrange("p (k q) -> p k q", k=8)),
             reads=[psb[bk]], writes=[B("xT")])

    def x_tile(j, t):
        x_load(j, t)
        x_tr(j, t)

    def x_stage(j):
        for t in range(4):
            x_tile(j, t)

    def k_proj(j):
        tok0 = j * T
        wks = [wtake(PW[2]), wtake(PW[3])]
        for hp in range(4):
            wk, wkb = wks[hp // 2]
            bk = FOB[0][nextbank(0, 4)]
            def f(e, hp=hp, bk=bk, wk=wk):
                ins = None
                for kt in range(8):
                    ins = e.matmul(PSB(bk), lhsT=wk[:, kt, (hp % 2) * 128:(hp % 2 + 1) * 128], rhs=xT[:, kt, :], start=(kt == 0), stop=(kt == 7))
                return ins
            P.op("pe", f, reads=[wkb, B("xT")], writes=[psb[bk]])
            for hh in range(2):
                h = 2 * hp + hh
                s2 = h % 4
                act(kTs[:, s2, :], PSB(bk)[hh * 64:(hh + 1) * 64, :], AF.Copy, [psb[bk]], [B("kTs%d" % s2)])
                for a_ in range(2):
                    P.op("dve", lambda e, s2=s2, a_=a_: e.reduce_sum(out=ksum[:, s2, a_:a_ + 1], in_=kTs[:, s2, a_ * 256:(a_ + 1) * 256], axis=AX.X),
                         reads=[B("kTs%d" % s2)] + ([B("ksum%d" % s2)] if a_ else []), writes=[B("ksum%d" % s2)])
                P.op("dve", lambda e, h=h, s2=s2, j=j: e.tensor_scalar(out=kmT[:, h, 2 * j:2 * j + 2], in0=ksum[:, s2, :], scalar1=1.0 / MOBA_BLOCK,
                                                                scalar2=None, op0=ALU.mult),
                     reads=[B("ksum%d" % s2)], writes=[B("kmT")])
                dma("act", ks_d[h, :, tok0:tok0 + T], kTs[:, s2, :], [B("kTs%d" % s2)], [B("ksd%d" % h)])
        issue_load()
        issue_load()

    def q_proj(j):
        wqs = [wtake(PW[0]), wtake(PW[1])]
        for hp in range(4):
            wq, wqb = wqs[hp // 2]
            bk = FOB[1][hp]
            def f(e, hp=hp, bk=bk, wq=wq):
                ins = None
                for kt in range(8):
                    ins = e.matmul(PSB(bk), lhsT=wq[:, kt, (hp % 2) * 128:(hp % 2 + 1) * 128], rhs=xT[:, kt, :], start=(kt == 0), stop=(kt == 7))
                return ins
            P.op("pe", f, reads=[wqb, B("xT")], writes=[psb[bk]])
            act(qaug[0:64, 2 * hp, :], PSB(bk)[0:64, :], AF.Copy, [psb[bk]], [B("qa%d" % (2 * hp))])
            act(qaug[0:64, 2 * hp + 1, :], PSB(bk)[64:128, :], AF.Copy, [psb[bk]], [B("qa%d" % (2 * hp + 1))])
        issue_load()
        issue_load()

    x_stage(0)
    try:
      stop_here("setup")
      for j in range(NCH):
          tok0 = j * T
          if j == 0:
              k_proj(j)
          if j == 0:
              q_proj(0)
          wvs = [wtake(PW[4]), wtake(PW[5])]
          for t in range(4):
              bk = nextbank()
              def f(e, t=t, bk=bk, wvs=wvs):
                  ins = None
                  for c_ in range(2):
                      for kt in range(8):
                          ins = e.matmul(PSB(bk)[:, c_ * 256:(c_ + 1) * 256], lhsT=xT[:, kt, t * 128:(t + 1) * 128], rhs=wvs[c_][0][:, kt, :],
                                         start=(kt == 0), stop=(kt == 7))
                  return ins
              P.op("pe", f, reads=[wvs[0][1], wvs[1][1], B("xT")], writes=[psb[bk]])
              act(vtmp[:, :, t, 0:64], PSB(bk).rearrange("p (h d) -> p h d", h=8), AF.Copy, [psb[bk]], [B("vtmp")])
          issue_load()
          issue_load()
          dma("act", vs_d[:, :, 4 * j:4 * j + 4, :].rearrange("h p t c -> p h t c"), vtmp[:],
              [B("vtmp")], [B("vsd")])
          nk = (j + 1) * 4

          def stage_load(h):
              b_ = h % 2
              dma("act", kst[0:64, b_, 0:nk * 128], ks_d[h, :, 0:nk * 128], [B("ksd%d" % h)], [B("kst%d" % b_)])
              dma("act", vst[:, b_, 0:nk, :], vs_d[h, :, 0:nk, :], [B("vsd")], [B("vst%d" % b_)])
          stage_load(0)
          def fgate(e):
              ins = None
              for h in range(8):
                  for t in range(4):
                      ins = e.matmul(PSB(5)[:, h * 64 + t * 16:h * 64 + (t + 1) * 16], lhsT=qaug[0:64, h, t * 128:(t + 1) * 128], rhs=kmT[:, h, :],
                                     start=True, stop=True)
              return ins
          P.op("pe", fgate, reads=[B("qa%d" % h) for h in range(8)] + [B("kmT")], writes=[psb[5]])
          pmq = pmt[:, 2 * j, :]
          pm_b = bass.AP(tensor=pmq.tensor, offset=pmq.offset, ap=[list(pmq.ap[0]), [16, 2], [0, 2], [1, 16]])
          owq = ownt[:, 2 * j, :]
          own_b = bass.AP(tensor=owq.tensor, offset=owq.offset, ap=[list(owq.ap[0]), [16, 2], [0, 2], [1, 16]])
          def chain(h):
              g2 = h % 2
              P.op("dve", lambda e, h=h, g2=g2, pm_b=pm_b: e.tensor_tensor(out=t1[:, g2, :, :].rearrange("p (a b) n -> p a b n", a=2),
                                                               in0=PSB(5)[:, h * 64:(h + 1) * 64].rearrange("p (a b n) -> p a b n", a=2, b=2),
                                                               in1=pm_b, op=ALU.add),
                   reads=[psb[5], B("pmt")], writes=[B("t1_%d" % g2)])
              for t in range(4):
                  P.op("dve", lambda e, g2=g2, t=t: e.max(out=m8[:, g2, t, :], in_=t1[:, g2, t, :]), reads=[B("t1_%d" % g2)], writes=[B("m8_%d" % g2)])
              P.op("dve", lambda e, g2=g2: e.tensor_scalar(out=thr[:, g2, :], in0=m8[:, g2, :, 2], scalar1=-1e29, scalar2=None, op0=ALU.max),
                   reads=[B("m8_%d" % g2)], writes=[B("thr%d" % g2)])
              for t in range(4):
                  P.op("dve", lambda e, g2=g2, t=t: e.tensor_scalar(out=sel[:, g2, t, :], in0=t1[:, g2, t, :], scalar1=thr[:, g2, t:t + 1], scalar2=None,
                                                                  op0=ALU.is_ge),
                       reads=[B("t1_%d" % g2), B("thr%d" % g2)], writes=[B("sel%d" % g2)])
              P.op("dve", lambda e, g2=g2, own_b=own_b: e.tensor_tensor(out=sel[:, g2, :, :].rearrange("p (a b) n -> p a b n", a=2),
                                                          in0=sel[:, g2, :, :].rearrange("p (a b) n -> p a b n", a=2), in1=own_b, op=ALU.add),
                   reads=[B("sel%d" % g2), B("ownt")], writes=[B("sel%d" % g2)])
              P.op("dve", lambda e, g2=g2, h=h: e.tensor_scalar(out=amb[:, h, :, :], in0=sel[:, g2, :, :], scalar1=-NEG, scalar2=None, op0=ALU.mult),
                   reads=[B("sel%d" % g2)], writes=[B("amb%d" % h)])

          def gate_T(h):
              bk = 6 + (h % 2)
              def f(e, h=h, bk=bk):
                  ins = None
                  for t in range(4):
                      ins = e.transpose(out=PSBF(bk)[0:16, t * 128:(t + 1) * 128], in_=amb[:, h, t, :], identity=ident[:])
                  return ins
              P.op("pe", f, reads=[B("amb%d" % h), B("ident")], writes=[psb[bk]])
              P.op("dve", lambda e, h=h, bk=bk: e.tensor_copy(out=qaug[64:80, h, :], in_=PSBF(bk)[0:16, 0:512]),
                   reads=[psb[bk]], writes=[B("qa%d" % h)])
          chain(0)
          chain(1)
          wps = [wtake(PW[6]), wtake(PW[7])]
          for t in range(4):
              bk = nextbank(0, 4)
              def f(e, t=t, bk=bk, wps=wps):
                  ins = None
                  for c_ in range(2):
                      for kt in range(8):
                          ins = e.matmul(PSB(bk)[:, c_ * 256:(c_ + 1) * 256], lhsT=xT[:, kt, t * 128:(t + 1) * 128], rhs=wps[c_][0][:, kt, :],
                                         start=(kt == 0), stop=(kt == 7))
                  return ins
              P.op("pe", f, reads=[wps[0][1], wps[1][1], B("xT")], writes=[psb[bk]])
              act(ptm[:, 1 + t, :], PSB(bk), AF.Copy, [psb[bk]], [B("ptm%d" % (1 + t))])
          issue_load()
          issue_load()
          gate_T(0)
          gate_T(1)
          for h_ in range(2, 8):
              chain(h_)
          for g in range(4):
              bk = nextbank(0, 4)
              def f(e, g=g, bk=bk, j=j):
                  ins = None
                  for t in range(4):
                      o_ = PSB(bk)[:, t * 128:(t + 1) * 128]
                      if j == 0 and t == 0:
                          e.matmul(o_, lhsT=ptm[:, 1, g * 128:(g + 1) * 128], rhs=mtab[:, 8 + g, :], start=True, stop=False)
                          ins = e.matmul(o_, lhsT=ptm[:, 1, g * 128:(g + 1) * 128], rhs=mtab[:, 12 + g, :], start=False, stop=True)
                      else:
                          e.matmul(o_, lhsT=ptm[:, 1 + t, g * 128:(g + 1) * 128], rhs=mtab[:, 0 + g, :], start=True, stop=False)
                          ins = e.matmul(o_, lhsT=ptm[:, t, g * 128:(g + 1) * 128], rhs=mtab[:, 4 + g, :], start=False, stop=True)
                  return ins
              P.op("pe", f, reads=[B("ptm%d" % i) for i in range(5)] + [B("ptm0"), B("mtab")], writes=[psb[bk]])
              act(diffT[:, g, :], PSB(bk), AF.Copy, [psb[bk]], [B("diffT%d" % g)])
          for g in range(4):
              bk2 = nextbank(0, 4)
              P.op("pe", lambda e, g=g, bk2=bk2: e.matmul(PSB(bk2), lhsT=wpool[:, g, :], rhs=diffT[:, g, :], start=True, stop=True),
                   reads=[B("wpool"), B("diffT%d" % g)], writes=[psb[bk2]])
              act(ypgT[:, g, :], PSB(bk2), AF.Copy, [psb[bk2], B("prm")], [B("ypgT")], scale=psc(g))
          P.op("pool", lambda e: e.tensor_copy(out=ptm[:, 0, :], in_=ptm[:, 4, :]), reads=[B("ptm4")], writes=[B("ptm0")])
          stop_here("gate")
          nkt = 4 * j + 4
          nun = 2 * j + 2
          units = [(h, ui_) for h in range(8) for ui_ in range(nun)]

          def unit_tiles(u):
              h, ui_ = u
              sbk = 2 * (ui_ % 2)
              return [(2 * ui_ + d_, sbk + d_) for d_ in range(2)]

          def emit_S(u):
              h, ui_ = u
              b_ = h % 2
              tiles = unit_tiles(u)
              def f(e, tiles=tiles, h=h, b_=b_, j=j):
                  ins = None
                  for kt, bk in tiles:
                      i_ = kt - 4 * j
                      c0 = 128 * i_ if i_ > 0 else 0
                      ins = e.matmul(PSB(bk)[:, c0:T], lhsT=kst[0:80, b_, kt * 128:(kt + 1) * 128], rhs=qaug[0:80, h, c0:T],
                                     start=True, stop=(i_ < 0))
                      if i_ >= 0:
                          nd = 2 if i_ < 3 else 1
                          for ty in range(nd):
                              cs = c0 + 128 * ty
                              e.matmul(PSB(bk)[:, cs:cs + 128], lhsT=ident[:], rhs=dth[:, 2 * h + ty, :], start=False, stop=False)
                              ins = e.matmul(PSB(bk)[:, cs:cs + 128], lhsT=ident[:], rhs=dtl[:, 2 * h + ty, :], start=False,
                                             stop=(ty == nd - 1))
                  return ins
              P.op("pe", f, reads=[B("kst%d" % b_), B("qa%d" % h), B("dtab"), B("ident")], writes=[psb[bk] for _, bk in tiles])

          def emit_exp(u):
              tiles = unit_tiles(u)
              sbk = tiles[0][1]
              act(pT[:, sbk:sbk + 2, :], psum[:, sbk:sbk + 2, :], AF.Exp, [psb[sbk], psb[sbk + 1]],
                  [B("pT%d" % sbk), B("pT%d" % (sbk + 1))], scale=0.125)

          def emit_PV(u):
              h, ui_ = u
              b_ = h % 2
              ob = 4 + h % 2
              tiles = unit_tiles(u)
              def f(e, tiles=tiles, b_=b_, ob=ob, j=j, nkt=nkt):
                  ins = None
                  for kt, bk in tiles:
                      i_ = kt - 4 * j
                      c0 = 128 * i_ if i_ > 0 else 0
                      ins = e.matmul(PSB(ob)[0:65, c0:T], lhsT=vst[:, b_, kt, :], rhs=pT[:, bk, c0:T], start=(kt == 0), stop=(kt == nkt - 1))
                  return ins
              P.op("pe", f, reads=[B("vst%d" % b_)] + [B("pT%d" % bk) for _, bk in tiles], writes=[psb[ob]])

          def finish_a(h):
              ob = 4 + h % 2
              hb2 = h % 2
              rec_ = lr_[:, (2 + hb2) * T:(3 + hb2) * T]
              P.op("dve", lambda e, rec_=rec_, ob=ob: e.reciprocal(out=rec_, in_=PSB(ob)[64:65, :]), reads=[psb[ob]], writes=[B("rec%d" % hb2)])

          def finish_b(h):
              ob = 4 + h % 2
              hb2 = h % 2
              rec_ = lr_[:, (2 + hb2) * T:(3 + hb2) * T]
              bb = 6 + hb2
              hl_ = bcs_full[64:65, hb2, :].bitcast(BF16)
              P.op("dve", lambda e, rec_=rec_, hl_=hl_: e.tensor_copy(out=hl_[:, 0:T], in_=rec_), reads=[B("rec%d" % hb2)], writes=[B("hl%d" % hb2)])
              P.op("dve", lambda e, rec_=rec_, hl_=hl_: e.tensor_tensor(out=hl_[:, T:2 * T], in0=rec_, in1=hl_[:, 0:T], op=ALU.subtract),
                   reads=[B("rec%d" % hb2), B("hl%d" % hb2)], writes=[B("hl%d" % hb2)])
              def fbc(e, hl_=hl_, bb=bb):
                  e.matmul(PSB(bb)[0:64, :], lhsT=onesb[64:65, :], rhs=hl_[:, 0:T], start=True, stop=False)
                  return e.matmul(PSB(bb)[0:64, :], lhsT=onesb[64:65, :], rhs=hl_[:, T:2 * T], start=False, stop=True)
              P.op("pe", fbc, reads=[B("onesb"), B("hl%d" % hb2)], writes=[psb[bb]])
              P.op("dve", lambda e, hb2=hb2, bb=bb: e.tensor_copy(out=bcs[:, hb2, :], in_=PSB(bb)[0:64, :]), reads=[psb[bb]], writes=[B("bcs%d" % hb2)])
              hp_ = h // 2
              if h % 2 == 0:
                  P.op("dve", lambda e, hp_=hp_, hb2=hb2, ob=ob: e.tensor_tensor(out=attnT[:, hp_, :], in0=PSB(ob)[0:64, :], in1=bcs[:, hb2, :], op=ALU.mult),
                       reads=[psb[ob], B("bcs%d" % hb2)], writes=[B("attnT%d" % h)])
              else:
                  P.op("dve", lambda e, hp_=hp_, hb2=hb2, ob=ob: e.tensor_tensor(out=attnT[:, 4 + hp_, :], in0=PSB(ob)[0:64, :], in1=bcs[:, hb2, :], op=ALU.mult),
                       reads=[psb[ob], B("bcs%d" % hb2)], writes=[B("attnTo%d" % h)])
                  P.op("pool", lambda e, hp_=hp_: e.tensor_copy(out=attnT_full[64:128, hp_, :], in_=attnT_full[0:64, 4 + hp_, :]),
                       reads=[B("attnTo%d" % h)], writes=[B("attnT%d" % h)])

          emit_S(units[0])
          if len(units) > 1:
              emit_S(units[1])
          pend = []
          DELAY = min(5, nun - 1)
          for ui, u in enumerate(units):
              h = u[0]
              first_of_head = (ui % nun == 0)
              last_of_head = (ui % nun == nun - 1)
              if first_of_head and 1 <= h and h + 1 < 8:
                  gate_T(h + 1)
              if (ui % nun == min(2, nun - 2)) and h + 1 < 8:
                  stage_load(h + 1)
              emit_exp(u)
              if ui + 2 < len(units):
                  emit_S(units[ui + 2])
              emit_PV(u)
              for pe_ in pend:
                  pe_[1] -= 1
              while pend and pend[0][1] <= 0:
                  finish_b(pend.pop(0)[0])
              if last_of_head:
                  finish_a(h)
                  pend.append([h, DELAY])
          while pend:
              finish_b(pend.pop(0)[0])
          stop_here("attn")
          for dp in range(4):
              wg0, wg0b = wtake(PW[8 + dp])
              wg1, wg1b = wtake(PW[12 + dp])
              wba, wbab = wtake(PBA[dp])
              wbp, wbpb = wtake(PBP[dp])
              for ci in range(2):
                  i_ = dp * 2 + ci
                  k4 = i_ % 2
                  bya, byp, bg0, bg1 = [4 * k4 + x_ for x_ in range(4)]
                  def fyp(e, ci=ci, b=byp, wbp=wbp):
                      ins = None
                      for g in range(4):
                          ins = e.matmul(PSB(b), lhsT=wbp[:, g, ci * 128:(ci + 1) * 128], rhs=ypgT[:, g, :], start=(g == 0), stop=(g == 3))
                      return ins
                  P.op("pe", fyp, reads=[wbpb, B("ypgT")], writes=[psb[byp]])
                  for (wg, wgb, b, si) in ((wg0, wg0b, bg0, 0), (wg1, wg1b, bg1, 1)):
                      def fg(e, ci=ci, b=b, wg=wg):
                          ins = None
                          for kt in range(8):
                              ins = e.matmul(PSB(b), lhsT=wg[:, kt, ci * 128:(ci + 1) * 128], rhs=xT[:, kt, :], start=(kt == 0), stop=(kt == 7))
                          return ins
                      P.op("pe", fg, reads=[wgb, B("xT")], writes=[psb[b]])
                      act(sg[:, si, :], PSB(b), AF.Sigmoid, [psb[b]], [B("sg%d" % si)])
                  def fya(e, ci=ci, b=bya, wba=wba):
                      ins = None
                      for hp_ in range(4):
                          ins = e.matmul(PSB(b), lhsT=wba[:, hp_, ci * 128:(ci + 1) * 128], rhs=attnT_full[:, hp_, :], start=(hp_ == 0), stop=(hp_ == 3))
                      return ins
                  P.op("pe", fya, reads=[wbab] + [B("attnT%d" % h) for h in range(8)], writes=[psb[bya]])
                  P.op("dve", lambda e, b=byp: e.tensor_tensor(out=mm[:, 1, :], in0=sg[:, 1, :], in1=PSB(b), op=ALU.mult),
                       reads=[psb[byp], B("sg1")], writes=[B("mm1")])
                  P.op("dve", lambda e, b=bya: e.tensor_tensor(out=mm[:, 0, :], in0=sg[:, 0, :], in1=PSB(b), op=ALU.mult),
                       reads=[psb[bya], B("sg0")], writes=[B("mm0")])
                  P.op("pool", lambda e, i_=i_: e.tensor_tensor(out=mixedT[:, i_, :], in0=mm[:, 0, :], in1=mm[:, 1, :], op=ALU.add),
                       reads=[B("mm0"), B("mm1")], writes=[B("mixedT")] + QA)
              for _ in range(4):
                  issue_load()
          stop_here("merge")
          wos = [wtake(PWO[q_]) for q_ in range(4)]
          load_ln(1)
          xrb = [xst[:, 0, :], stmp[:].rearrange("p a b -> p (a b)")]
          xrbn = ["xst0", "xst1"]
          h1bb = [h1b[:, 0, :], mm[:, 0, :].bitcast(BF16)]
          h1bn = ["h1b0", "mm0"]

          def ln1_mm(t):
              for hf in range(2):
                  bk = 2 * t + hf
                  def f(e, t=t, bk=bk, hf=hf, wos=wos):
                      ins = None
                      for c_ in range(2):
                          wo = wos[2 * hf + c_][0]
                          for kt in range(8):
                              ins = e.matmul(PSB(bk)[:, c_ * 256:(c_ + 1) * 256], lhsT=mixedT[:, kt, t * 128:(t + 1) * 128], rhs=wo[:, kt, :],
                                             start=(kt == 0), stop=(kt == 7))
                      return ins
                  P.op("pe", f, reads=[wos[2 * hf][1], wos[2 * hf + 1][1], B("mixedT")] + QA, writes=[psb[bk]])

          def ln1_A(t):
              s2 = t % 2
              r0 = tok0 + t * 128
              dma("pool", xrb[s2], I["x"][r0:r0 + 128, :], [], [B(xrbn[s2])])
              hb = B("h1_%d" % t)
              for hf in range(2):
                  bk = 2 * t + hf
                  P.op("dve", lambda e, t=t, hf=hf, s2=s2, bk=bk: e.scalar_tensor_tensor(
                      out=h1[:, t, hf * 512:(hf + 1) * 512], in0=xrb[s2][:, hf * 512:(hf + 1) * 512], scalar=ALPHA, in1=PSB(bk),
                      op0=ALU.mult, op1=ALU.add), reads=[psb[bk], B(xrbn[s2]), hb], writes=[hb])
              ln_stats(t, t % 2)

          def ln1_B(t):
              s2 = t % 2
              hb = B("h1_%d" % t)
              ln_affine(t, t % 2)
              act(h1bb[s2], h1[:, t, :], AF.Copy, [hb], [B(h1bn[s2])])
              bk = 2 * t
              def f(e, s2=s2, bk=bk):
                  ins = None
                  for kt in range(8):
                      ins = e.transpose(out=PSBF(bk)[:, kt * 128:(kt + 1) * 128], in_=h1bb[s2][:, kt * 128:(kt + 1) * 128], identity=ident[:])
                  return ins
              P.op("pe", f, reads=[B(h1bn[s2]), B("ident")], writes=[psb[bk]])

          def ln1_C(t):
              bk = 2 * t
              act(h1T[:, :, t * 128:(t + 1) * 128], PSBF(bk).rearrange("p (k q) -> p k q", k=8), AF.Copy, [psb[bk]], [B("xT")])

          for t in range(4):
              ln1_mm(t)
          for step in range(6):
              if step < 4:
                  ln1_A(step)
              if 0 <= step - 1 < 4:
                  ln1_B(step - 1)
              if 0 <= step - 2 < 4:
                  ln1_C(step - 2)
          for _ in range(4):
              issue_load()
          stop_here("ln1")
          if j + 1 < NCH:
              x_load(j + 1, 0)
              x_dma(j + 1, 1)
          for m_ in range(11):
              wa, wab = wtake(PA[m_])
              wu, wub = wtake(PU[m_])
              for ci in range(2):
                  f_ = m_ * 2 + ci
                  s2 = 0
                  ba = (2 * f_) % 4
                  bu = (2 * f_ + 1) % 4
                  for (w_, wb_, b) in ((wa, wab, ba), (wu, wub, bu)):
                      def fm(e, ci=ci, b=b, w_=w_):
                          ins = None
                          for kt in range(8):
                              ins = e.matmul(PSB(b), lhsT=w_[:, kt, ci * 128:(ci + 1) * 128], rhs=h1T[:, kt, :], start=(kt == 0), stop=(kt == 7))
                          return ins
                      P.op("pe", fm, reads=[wb_, B("xT")], writes=[psb[b]])
                  ab = B("asb%d" % s2)
                  P.op("pool", lambda e, f_=f_, s2=s2: e.tensor_copy(out=asb[:, s2, 0:2], in_=halo[:, f_, :]), reads=[B("halo")], writes=[ab])
                  act(asb[:, s2, 2:T + 2], PSB(ba), AF.Copy, [psb[ba], ab], [ab])
                  P.op("pool", lambda e, f_=f_, s2=s2: e.tensor_copy(out=halo[:, f_, :], in_=asb[:, s2, T:T + 2]), reads=[ab], writes=[B("halo")])
                  P.op("dve", lambda e, f_=f_, s2=s2: e.tensor_scalar(out=o1[:, s2, :], in0=asb[:, s2, 2:T + 2], scalar1=cw(f_, 2), scalar2=None, op0=ALU.mult),
                       reads=[ab, B("prm")], writes=[B("o1_%d" % s2)])
                  P.op("dve", lambda e, f_=f_, s2=s2: e.scalar_tensor_tensor(out=o2[:, s2, :], in0=asb[:, s2, 1:T + 1], scalar=cw(f_, 1), in1=o1[:, s2, :],
                                                                            op0=ALU.mult, op1=ALU.add),
                       reads=[ab, B("prm"), B("o1_%d" % s2)], writes=[B("o2_%d" % s2)])
                  P.op("dve", lambda e, f_=f_, s2=s2: e.scalar_tensor_tensor(out=o1[:, s2, :], in0=asb[:, s2, 0:T], scalar=cw(f_, 0), in1=o2[:, s2, :],
                                                                            op0=ALU.mult, op1=ALU.add),
                       reads=[ab, B("prm"), B("o2_%d" % s2), B("o1_%d" % s2)], writes=[B("o1_%d" % s2)])
                  act(gl[:, s2, :], o1[:, s2, :], AF.Gelu, [B("o1_%d" % s2), B("prm")], [B("gl%d" % s2)], bias=cbv(f_), scale=1.0)
                  P.op("dve", lambda e, f_=f_, s2=s2, bu=bu: e.tensor_tensor(out=guT[:, f_, :], in0=gl[:, s2, :], in1=PSB(bu), op=ALU.mult),
                       reads=[B("gl%d" % s2), psb[bu]], writes=[B("guT%d" % f_)])
              issue_load()
              issue_load()
          stop_here("ffnin")
          load_ln(2)
          for tp_ in range(2):
              accs = {}
              for m_ in range(11):
                  wf, wfb = wtake(PFO[m_])
                  nk_ = 2
                  def fo(e, m_=m_, nk_=nk_, wf=wf, tp_=tp_):
                      ins = None
                      for kl in range(nk_):
                          kt = m_ * 2 + kl
                          for tt_ in range(2):
                              t = 2 * tp_ + tt_
                              for hf in range(2):
                                  b = FOB[tp_][2 * tt_ + hf]
                                  ins = e.matmul(PSB(b), lhsT=guT[:, kt, t * 128:(t + 1) * 128], rhs=wf[:, kl, hf * 512:(hf + 1) * 512],
                                                 start=(kt == 0), stop=(kt == NFF - 1))
                      return ins
                  P.op("pe", fo, reads=[wfb, B("guT%d" % (2 * m_)), B("guT%d" % (2 * m_ + 1))], writes=[psb[FOB[tp_][x_]] for x_ in range(4)])
                  issue_load()
                  if tp_ == 0 and m_ % 2 == 1 and 3 <= m_ < 10 and j + 1 < NCH:
                      xt_ = (m_ - 3) // 2
                      x_tr(j + 1, xt_)
                      if xt_ + 1 < 4:
                          x_cast(j + 1, xt_ + 1)
                      if xt_ + 2 < 4:
                          x_dma(j + 1, xt_ + 2)
              for tt_ in range(2):
                  t = 2 * tp_ + tt_
                  hb = B("h1_%d" % t)
                  for hf in range(2):
                      b = FOB[tp_][2 * tt_ + hf]
                      P.op("dve", lambda e, t=t, hf=hf, b=b: e.scalar_tensor_tensor(
                          out=h1[:, t, hf * 512:(hf + 1) * 512], in0=h1[:, t, hf * 512:(hf + 1) * 512], scalar=ALPHA, in1=PSB(b),
                          op0=ALU.mult, op1=ALU.add), reads=[psb[b], hb], writes=[hb])
              if tp_ == 1 and j + 1 < NCH:
                  k_proj(j + 1)
                  q_proj(j + 1)
              for tt_ in range(2):
                  t = 2 * tp_ + tt_
                  hb = B("h1_%d" % t)
                  layer_norm(t, t % 2, 2, 3)
                  r0 = tok0 + t * 128
                  dma("pool", out[r0:r0 + 128, :], h1[:, t, :], [hb], [B("outd%d" % t)])
    except _Stop:
        pass
    P.wait_all("pool", [B("outd%d" % t) for t in range(4)])
    P.emit(nc)
    st.close()
    return nc


_CACHE = {}


def kernel(x, w_in, rel_bias, w_pool_group, pool_scale, w_branch_attn, w_branch_pool, w_out, ln1_g, ln1_b,
           w_ffn_in, conv_w, conv_b, w_ffn_out, ln2_g, ln2_b):
    f32 = lambda a: np.ascontiguousarray(np.asarray(a, dtype=np.float32))
    x = f32(x)
    nb, seq, _ = x.shape
    if "nc" not in _CACHE:
        _CACHE["nc"] = build(seq)
        _CACHE["consts"] = make_consts()
    nc = _CACHE["nc"]
    shared = {
        "w_in": f32(w_in)[0], "rel_bias": f32(rel_bias), "w_pool_group": f32(w_pool_group)[0],
        "pool_scale": f32(pool_scale)[0].reshape(4, 128), "w_branch_attn": f32(w_branch_attn)[0],
        "w_branch_pool": f32(w_branch_pool)[0], "w_out": f32(w_out)[0], "ln1_g": f32(ln1_g)[0].reshape(1, 1024),
        "ln1_b": f32(ln1_b)[0].reshape(1, 1024), "w_ffn_in": f32(w_ffn_in)[0],
        "conv_w": f32(conv_w)[0].reshape(3 * NFF, 128), "conv_b": f32(conv_b)[0].reshape(NFF, 128),
        "w_ffn_out": f32(w_ffn_out)[0], "ln2_g": f32(ln2_g)[0].reshape(1, 1024), "ln2_b": f32(ln2_b)[0].reshape(1, 1024),
    }
    shared.update(_CACHE["consts"])
    in_maps = [dict(shared, x=x[b]) for b in range(nb)]
    res = run_bass_kernel_spmd(nc, in_maps, core_ids=list(range(nb)))
    return np.stack([np.asarray(r["out"], dtype=np.float32) for r in res.results], axis=0)
```

```python
import math
import os
from contextlib import ExitStack

import numpy as np
import ml_dtypes
import concourse.bass as bass
import concourse.mybir as mybir
from concourse.bass_utils import run_bass_kernel_spmd

F32 = mybir.dt.float32
BF16 = mybir.dt.bfloat16
AF = mybir.ActivationFunctionType
ALU = mybir.AluOpType
AX = mybir.AxisListType

D_MODEL = 1024
SEQ = 4096
N_HEADS = 8
HEAD_DIM = 64
MOBA_BLOCK = 256
POOL_WINDOWS = (2, 4, 8, 16)
N_BUCKETS = 32
MAX_DISTANCE = 128
D_FF = 2816
NFF = D_FF // 128
ALPHA = 2.0 ** 0.25
LN_EPS = 1e-5
NEG = -1e30
T = 512
ENGS = ("pe", "act", "dve", "pool", "sp")
SEM_LIMIT = 30000


class Buf:
    __slots__ = ("name", "w", "r")

    def __init__(self, name):
        self.name = name
        self.w = None
        self.r = {}


class Prog:
    def __init__(self):
        self.ops = {e: [] for e in ENGS}
        self.cnt = {}
        self.waited = {e: {} for e in ENGS}
        self.keys = []
        self.epoch = {e: 0 for e in ENGS}

    def _bump(self, key, n):
        if key not in self.cnt:
            self.cnt[key] = 0
            self.keys.append(key)
        self.cnt[key] += n
        return (key, self.cnt[key])

    def _waits(self, eng, reads, writes):
        need = {}

        def add(ev):
            if ev is None:
                return
            k, v = ev
            if eng == "pe" and k.startswith("E_pe"):
                return
            if need.get(k, 0) < v:
                need[k] = v
        for b in reads:
            add(b.w)
        for b in writes:
            add(b.w)
            for k, v in b.r.items():
                add((k, v))
        out = []
        wd = self.waited[eng]
        for k, v in need.items():
            if wd.get(k, 0) < v:
                wd[k] = v
                out.append((k, v))
        return out

    def _register(self, ev, reads, writes):
        k, v = ev
        for b in reads:
            if b.r.get(k, 0) < v:
                b.r[k] = v
        for b in writes:
            b.w = ev
            b.r = {}

    def op(self, eng, fn, reads=(), writes=()):
        waits = self._waits(eng, reads, writes)
        key = "E_%s_%d" % (eng, self.epoch[eng])
        if self.cnt.get(key, 0) >= SEM_LIMIT:
            self.epoch[eng] += 1
            key = "E_%s_%d" % (eng, self.epoch[eng])
        ev = self._bump(key, 1)
        self.ops[eng].append((fn, waits, ev, 1))
        self._register(ev, reads, writes)
        return ev

    def dma(self, q, fn, reads=(), writes=(), key=None):
        waits = self._waits(q, reads, writes)
        key = key or ("D_" + writes[0].name)
        ev = self._bump(key, 16)
        self.ops[q].append((fn, waits, ev, 16))
        self._register(ev, reads, writes)
        return ev

    def wait_all(self, eng, bufs):
        waits = self._waits(eng, bufs, bufs)
        self.ops[eng].append((None, waits, None, 0))

    def emit(self, nc):
        with ExitStack() as st:
            sems = {}
            for k in self.keys:
                sems[k] = st.enter_context(nc.semaphore(k))
            block = st.enter_context(nc.Block())

            def run(eng_name):
                def body(e):
                    for fn, waits, ev, n in self.ops[eng_name]:
                        for k, v in waits:
                            e.wait_ge(sems[k], v)
                        if fn is not None:
                            ins = fn(e)
                            ins.then_inc(sems[ev[0]], n)
                return body

            block.tensor(run("pe"))
            block.scalar(run("act"))
            block.vector(run("dve"))
            block.gpsimd(run("pool"))
            block.sync(run("sp"))


def _rel_bucket_np(d):
    d = np.asarray(d, dtype=np.int64)
    max_exact = N_BUCKETS // 2
    n = np.maximum(d, 0)
    nf = np.maximum(n, 1).astype(np.float32)
    val = (np.log(nf / np.float32(max_exact)) / np.float32(math.log(MAX_DISTANCE / max_exact))
           * np.float32(N_BUCKETS - max_exact)).astype(np.float32)
    large = max_exact + val.astype(np.int32)
    large = np.minimum(large, N_BUCKETS - 1)
    return np.where(n < max_exact, n, large).astype(np.int64)


def make_consts():
    bf = ml_dtypes.bfloat16
    c = {}
    c["c_ident"] = np.eye(128, dtype=np.float32).astype(bf)
    c["c_identf"] = np.eye(128, dtype=np.float32)
    c["c_jmat"] = np.ascontiguousarray(np.eye(128, dtype=np.float32)[::-1])
    oh = np.zeros((33, 384), np.float32)
    for i in range(383):
        d = i - 127
        if d < 0:
            oh[32, i] = NEG
        else:
            oh[int(_rel_bucket_np(d)), i] += 1.0
            oh[N_BUCKETS - 1, i] -= 1.0
    c["c_oh"] = oh
    m = np.zeros((128, 16, 128), np.float32)
    tp = np.arange(128)[:, None]
    tt = np.arange(128)[None, :]
    for g, w in enumerate(POOL_WINDOWS):
        band = ((tt - tp) >= 0) & ((tt - tp) < w)
        m[:, 0 * 4 + g, :] = band * (1.0 / w) - (tp == tt)
        bandp = (tt + 128 - tp) < w
        m[:, 1 * 4 + g, :] = bandp * (1.0 / w)
        cnt = np.minimum(tt + 1, w).astype(np.float32)
        mf = band * (1.0 / cnt) - (tp == tt)
        hi = mf.astype(bf).astype(np.float32)
        m[:, 2 * 4 + g, :] = hi
        m[:, 3 * 4 + g, :] = mf - hi
    c["c_mtab"] = m.astype(bf)
    nb = SEQ // MOBA_BLOCK
    pm = np.zeros((128, nb, 16), np.float32)
    own = np.zeros((128, nb, 16), np.float32)
    for qb in range(nb):
        for n in range(16):
            pm[:, qb, n] = 0.0 if n < qb else NEG
            own[:, qb, n] = (1.0 if n == qb else 0.0) - 1.0
    c["c_pm"] = pm
    c["c_own"] = own
    kind = np.zeros((16, SEQ), np.float32)
    for n in range(16):
        kind[n, n * MOBA_BLOCK:(n + 1) * MOBA_BLOCK] = 1.0
    c["c_kind"] = kind.astype(bf)
    return c


CONST_SPECS = [("c_ident", [128, 128], BF16), ("c_identf", [128, 128], F32), ("c_jmat", [128, 128], F32),
               ("c_oh", [33, 384], F32), ("c_mtab", [128, 16, 128], BF16), ("c_pm", [128, 16, 16], F32),
               ("c_own", [128, 16, 16], F32), ("c_kind", [16, SEQ], BF16)]

INPUT_SPECS = [("x", [SEQ, D_MODEL]), ("w_in", [D_MODEL, 4096]), ("rel_bias", [8, 32]),
               ("w_pool_group", [4, 128, 128]), ("pool_scale", [4, 128]), ("w_branch_attn", [512, 1024]),
               ("w_branch_pool", [512, 1024]), ("w_out", [1024, 1024]), ("ln1_g", [1, 1024]), ("ln1_b", [1, 1024]),
               ("w_ffn_in", [1024, 2 * D_FF]), ("conv_w", [3 * NFF, 128]), ("conv_b", [NFF, 128]),
               ("w_ffn_out", [D_FF, 1024]), ("ln2_g", [1, 1024]), ("ln2_b", [1, 1024])]


def build(seq=SEQ, dbg=False):
    NCH = seq // T
    nc = bass.Bass("TRN2", target_bir_lowering=False)
    I = {}
    for name, shape in INPUT_SPECS:
        shp = [seq, D_MODEL] if name == "x" else shape
        I[name] = nc.dram_tensor(name, shp, F32, kind="ExternalInput").ap()
    for name, shape, dt in CONST_SPECS:
        I[name] = nc.dram_tensor(name, shape, dt, kind="ExternalInput").ap()
    out = nc.dram_tensor("out", [seq, D_MODEL], F32, kind="ExternalOutput").ap()
    NPIECE = 16 + 4 + 4 + 4 + 22 + 11
    wsc = nc.dram_tensor("wsc", [NPIECE, 128, 2048], BF16, kind="Internal").ap()
    ks_d = nc.dram_tensor("ks_d", [8, 64, seq], BF16, kind="Internal").ap()
    vs_d = nc.dram_tensor("vs_d", [8, 128, seq // 128, 65], BF16, kind="Internal").ap()
    tsc = nc.dram_tensor("tsc", [8, 384], F32, kind="Internal").ap()

    P = Prog()
    st = ExitStack()
    bufs = {}
    STOP = os.environ.get("MK_STOP", "")

    class _Stop(Exception):
        pass

    def stop_here(tag):
        if STOP == tag:
            raise _Stop()

    def B(name):
        if name not in bufs:
            bufs[name] = Buf(name)
        return bufs[name]

    def sb(name, shape, dt):
        return st.enter_context(nc.sbuf_tensor(name, shape, dt))

    RING = 8
    ring = sb("ring", [128, RING, 2048], BF16)
    ident = sb("ident", [128, 128], BF16)
    identf = sb("identf", [128, 128], F32)
    jmat = sb("jmat", [128, 128], F32)
    ohs = sb("ohs", [33, 384], F32)
    rbT = sb("rbT", [33, 8], F32)
    tsb = sb("tsb", [8, 384], F32)
    dth = sb("dth", [128, 16, 128], BF16)
    dtl = sb("dtl", [128, 16, 128], BF16)
    mtab = sb("mtab", [128, 16, 128], BF16)
    pmt = sb("pmt", [128, 16, 16], F32)
    ownt = sb("ownt", [128, 16, 16], F32)
    prow = sb("prow", [92, 128], F32)
    prm = sb("prm", [128, 92], F32)
    lnc = sb("lnc", [128, 2, 1024], F32)
    wpool = sb("wpool", [128, 4, 128], BF16)
    ones64 = sb("ones64", [128, 64], F32)
    kmT = sb("kmT", [64, 8, 16], BF16)
    xst = sb("xst", [128, 1, 1024], F32)
    xbt = sb("xbt", [128, 1, 1024], BF16)
    xT = sb("xT", [128, 8, T], BF16)
    h1T = xT
    qaug_full = sb("qaug", [128, 8, T], BF16)
    qaug = qaug_full[0:80]
    kTs = sb("kTs", [64, 4, T], BF16)
    ksum = sb("ksum", [64, 4, 2], F32)
    vtmp = sb("vtmp", [128, 8, 4, 65], BF16)
    ptm = sb("ptm", [128, 5, 512], BF16)
    diffT = sb("diffT", [128, 4, T], BF16)
    ypgT = sb("ypgT", [128, 4, T], BF16)
    attnT_full = sb("attnT", [128, 8, T], BF16)
    attnT = attnT_full[0:64]
    lr_ = attnT_full[64:65].rearrange("p a b -> p (a b)").bitcast(F32)
    kst = sb("kst", [80, 2, seq], BF16)
    vst = sb("vst", [128, 2, seq // 128, 65], BF16)
    pT = sb("pT", [128, 4, T], BF16)
    stmp = sb("stmp", [128, 2, T], F32)
    t1 = sb("t1", [128, 2, 4, 16], F32)
    m8 = sb("m8", [128, 2, 4, 8], F32)
    thr = sb("thr", [128, 2, 4], F32)
    sel = sb("sel", [128, 2, 4, 16], F32)
    amb = sb("amb", [128, 8, 4, 16], BF16)
    bcs_full = sb("bcs", [128, 2, T], F32)
    bcs = bcs_full[0:64]
    onesb = sb("onesb", [128, 64], BF16)
    sg = sb("sg", [128, 2, T], F32)
    mm = sb("mm", [128, 2, T], F32)
    h1 = sb("h1", [128, 4, 1024], F32)
    h1b = sb("h1b", [128, 1, 1024], BF16)
    stat = sb("stat", [128, 2, 2, 6], F32)
    mv = sb("mv", [128, 2, 2], F32)
    sd = sb("sd", [128, 2], F32)
    rstd = sb("rstd", [128, 2], F32)
    asb = sb("asb", [128, 1, T + 2], F32)
    halo = sb("halo", [128, NFF, 2], F32)
    o1 = sb("o1", [128, 1, T], F32)
    o2 = sb("o2", [128, 1, T], F32)
    gl = sb("gl", [128, 1, T], F32)
    guT = sb("guT", [128, NFF, T], BF16)
    hst = guT[:, 0:8, :].rearrange("p a b -> p (a b)").bitcast(F32).rearrange("p (a b) -> p a b", a=16)
    psum = st.enter_context(nc.psum_tensor("psum", [128, 8, 512], F32))
    mixedT = qaug_full
    QA = [B("qa%d" % h) for h in range(8)]
    if os.environ.get("MK_PROBE"):
        try:
            sb("probe", [128, 60000], F32)
        except AssertionError as ex:
            print("SBUF probe:", str(ex)[:200])

    def PSB(b):
        return psum[:, b, :]

    def PSBF(b):
        return psum[:, b, :].bitcast(BF16)

    psb = [B("ps%d" % i) for i in range(8)]
    rot = {"i": 0}

    def nextbank(lo=0, hi=8):
        b = lo + rot["i"] % (hi - lo)
        rot["i"] += 1
        return b

    def dma(q, o, i, reads, writes, key=None, **kw):
        P.dma(q, lambda e, o=o, i=i, kw=kw: e.dma_start(out=o, in_=i, **kw), reads=reads, writes=writes, key=key)

    def act(o, i, func, reads, writes, **kw):
        P.op("act", lambda e, o=o, i=i, kw=kw: e.activation(out=o, in_=i, func=func, **kw), reads=reads, writes=writes)

    for nm, t_, src in [("ident", ident, "c_ident"), ("identf", identf, "c_identf"), ("jmat", jmat, "c_jmat"),
                        ("ohs", ohs, "c_oh"), ("mtab", mtab, "c_mtab"), ("pmt", pmt, "c_pm"), ("ownt", ownt, "c_own")]:
        dma("sp", t_[:], I[src], [], [B(nm)])
    for b_ in range(2):
        dma("sp", kst[64:80, b_, :], I["c_kind"][:, 0:seq], [], [B("kst%d" % b_)])
    dma("sp", prow[0:66, :], I["conv_w"], [], [B("prow")], key="D_prow")
    dma("sp", prow[66:88, :], I["conv_b"], [], [B("prow")], key="D_prow")
    dma("sp", prow[88:92, :], I["pool_scale"], [], [B("prow")], key="D_prow")
    def load_ln(which):
        for k_, nm in enumerate(["ln%d_g" % which, "ln%d_b" % which]):
            dma("pool", lnc[:, k_, :], I[nm].partition_broadcast(128), [], [B("lnc%d" % k_)])
    dma("sp", rbT[0:32, :], I["rel_bias"].rearrange("h b -> b h"), [], [B("rbT")], allow_slow_non_contiguous=True)
    P.op("dve", lambda e: e.memset(rbT[32:33, :], 1.0), reads=[], writes=[B("rbT1")])
    P.op("dve", lambda e: e.memset(ones64[:], 1.0), writes=[B("ones64")])
    P.op("dve", lambda e: e.memset(onesb[:], 1.0), writes=[B("onesb")])
    P.op("dve", lambda e: e.memset(kmT[:], 0.0), writes=[B("kmT")])
    P.op("dve", lambda e: e.memset(halo[:], 0.0), writes=[B("halo")])
    P.op("dve", lambda e: e.memset(vtmp[:], 1.0), writes=[B("vtmp")])
    P.op("dve", lambda e: e.memset(ptm[:, 0, :], 0.0), writes=[B("ptm0")])
    dma("pool", wpool[:], I["w_pool_group"].rearrange("g c d -> c g d"), [], [B("wpool")])
    P.op("pe", lambda e: e.transpose(out=PSB(7)[:, 0:92], in_=prow[:], identity=identf[0:92, 0:92]),
         reads=[B("prow"), B("identf")], writes=[psb[7]])
    act(prm[:], PSB(7)[:, 0:92], AF.Copy, [psb[7]], [B("prm")])
    P.op("pe", lambda e: e.matmul(PSB(6)[0:8, 0:384], lhsT=rbT[0:33, 0:8], rhs=ohs[0:33, :], start=True, stop=True),
         reads=[B("rbT"), B("rbT1"), B("ohs")], writes=[psb[6]])
    act(tsb[:], PSB(6)[0:8, 0:384], AF.Copy, [psb[6]], [B("tsb")])
    dma("sp", tsc, tsb[:], [B("tsb")], [B("tsc")])
    hank = bass.AP(tensor=tsc.tensor, offset=tsc.offset, ap=[[1, 128], [384, 8], [128, 2], [1, 128]])
    dma("sp", hst[:].rearrange("p (h t) q -> p h t q", t=2), hank, [B("tsc")], [B("guT%d" % i_) for i_ in range(8)])
    for grp in range(4):
        bk = grp
        def f(e, grp=grp, bk=bk):
            ins = None
            for k_ in range(4):
                idx = grp * 4 + k_
                ins = e.matmul(PSB(bk)[:, k_ * 128:(k_ + 1) * 128], lhsT=jmat[:], rhs=hst[:, idx, :], start=True, stop=True)
            return ins
        P.op("pe", f, reads=[B("jmat")] + [B("guT%d" % i_) for i_ in range(8)], writes=[psb[bk]])
        d8 = h1[:, grp, 0:512].rearrange("p (a q) -> p a q", a=4)
        act(d8, PSB(bk).rearrange("p (a q) -> p a q", a=4), AF.Copy, [psb[bk]], [B("h1_%d" % grp)], scale=8.0)
        P.op("dve", lambda e, grp=grp, d8=d8: e.tensor_copy(out=dth[:, grp * 4:(grp + 1) * 4, :], in_=d8), reads=[B("h1_%d" % grp)], writes=[B("dtab")])
        P.op("dve", lambda e, grp=grp, d8=d8: e.tensor_tensor(out=dtl[:, grp * 4:(grp + 1) * 4, :], in0=d8, in1=dth[:, grp * 4:(grp + 1) * 4, :], op=ALU.subtract),
             reads=[B("h1_%d" % grp), B("dtab")], writes=[B("dtab")])

    pieces = []

    def addp(src, npart, shape):
        pieces.append((src, npart, shape))
        return len(pieces) - 1

    w_in_v = I["w_in"].rearrange("(kt p) c -> p kt c", p=128)
    PW = [addp(w_in_v[:, :, c * 256:(c + 1) * 256], 128, (8, 256)) for c in range(16)]
    wba_v = I["w_branch_attn"].rearrange("(hp q) c -> q hp c", q=128)
    PBA = [addp(wba_v[:, :, c * 256:(c + 1) * 256], 128, (4, 256)) for c in range(4)]
    wbp_v = I["w_branch_pool"].rearrange("(g p) c -> p g c", p=128)
    PBP = [addp(wbp_v[:, :, c * 256:(c + 1) * 256], 128, (4, 256)) for c in range(4)]
    wout_v = I["w_out"].rearrange("(kt p) c -> p kt c", p=128)
    PWO = [addp(wout_v[:, :, c * 256:(c + 1) * 256], 128, (8, 256)) for c in range(4)]
    wfi_v = I["w_ffn_in"].rearrange("(kt p) c -> p kt c", p=128)
    PA = [addp(wfi_v[:, :, m_ * 256:(m_ + 1) * 256], 128, (8, 256)) for m_ in range(11)]
    PU = [addp(wfi_v[:, :, D_FF + m_ * 256:D_FF + (m_ + 1) * 256], 128, (8, 256)) for m_ in range(11)]
    wfo_v = I["w_ffn_out"].rearrange("(kt p) c -> p kt c", p=128)
    PFO = [addp(wfo_v[:, m_ * 2:m_ * 2 + 2, :], 128, (2, 1024)) for m_ in range(11)]
    assert len(pieces) == NPIECE
    chunk_seq = [PW[2], PW[3], PW[0], PW[1], PW[4], PW[5], PW[6], PW[7]]
    for dp in range(4):
        chunk_seq += [PW[8 + dp], PW[12 + dp], PBA[dp], PBP[dp]]
    chunk_seq += PWO
    for m_ in range(11):
        chunk_seq += [PA[m_], PU[m_]]
    chunk_seq += PFO + PFO
    full_seq = chunk_seq * NCH
    in_scratch = set()
    wstate = {"next": 0, "cur": 0}
    wsc_buf = [B("wsc_all")] * NPIECE
    ring_buf = [B("ring%d" % i) for i in range(RING)]

    def slot_view(slot, pi):
        src, npart, shape = pieces[pi]
        n = shape[0] * shape[1]
        return ring[0:npart, slot, 0:n].rearrange("p (a b) -> p a b", a=shape[0])

    def issue_load():
        s = wstate["next"]
        if s >= len(full_seq):
            return
        wstate["next"] += 1
        pi = full_seq[s]
        slot = s % RING
        src, npart, shape = pieces[pi]
        n = shape[0] * shape[1]
        if pi in in_scratch:
            dma("sp", ring[0:npart, slot, 0:n], wsc[pi, 0:npart, 0:n], [wsc_buf[pi]], [ring_buf[slot]])
        else:
            dma("pool", slot_view(slot, pi), src, [], [ring_buf[slot]], key="D_ringsw%d" % slot)
            dma("sp", wsc[pi, 0:npart, 0:n], ring[0:npart, slot, 0:n], [ring_buf[slot]], [wsc_buf[pi]])
            in_scratch.add(pi)

    def wget(pi):
        s = wstate["cur"]
        assert full_seq[s] == pi, (s, full_seq[s], pi)
        slot = s % RING
        return slot_view(slot, pi), ring_buf[slot]

    def wdone():
        wstate["cur"] += 1
        issue_load()

    def wtake(pi):
        v, b_ = wget(pi)
        wstate["cur"] += 1
        return v, b_

    for _ in range(RING):
        issue_load()

    cw = lambda f_, i_: prm[:, i_ * NFF + f_: i_ * NFF + f_ + 1]
    cbv = lambda f_: prm[:, 66 + f_: 67 + f_]
    psc = lambda g_: prm[:, 88 + g_: 89 + g_]

    def ln_stats(t, k):
        hb = B("h1_%d" % t)
        for hf in range(2):
            P.op("dve", lambda e, hf=hf: e.bn_stats(out=stat[:, k, hf, :], in_=h1[:, t, hf * 512:(hf + 1) * 512]),
                 reads=[hb], writes=[B("stat%d" % k)])
        P.op("dve", lambda e: e.bn_aggr(out=mv[:, k, :], in_=stat[:, k, :, :].rearrange("p a b -> p (a b)")),
             reads=[B("stat%d" % k)], writes=[B("mv%d" % k)])
        act(sd[:, k:k + 1], mv[:, k, 1:2], AF.Sqrt, [B("mv%d" % k)], [B("sd%d" % k)], bias=LN_EPS, scale=1.0)
        P.op("dve", lambda e: e.reciprocal(out=rstd[:, k:k + 1], in_=sd[:, k:k + 1]), reads=[B("sd%d" % k)], writes=[B("rstd%d" % k)])

    def ln_affine(t, k):
        hb = B("h1_%d" % t)
        src = h1[:, t, :]
        P.op("dve", lambda e: e.scalar_tensor_tensor(out=src, in0=src, scalar=mv[:, k, 0:1], in1=lnc[:, 0, :], op0=ALU.subtract, op1=ALU.mult),
             reads=[hb, B("mv%d" % k), B("lnc0")], writes=[hb])
        P.op("dve", lambda e: e.scalar_tensor_tensor(out=src, in0=src, scalar=rstd[:, k:k + 1], in1=lnc[:, 1, :], op0=ALU.mult, op1=ALU.add),
             reads=[hb, B("rstd%d" % k), B("lnc1")], writes=[hb])

    def layer_norm(t, k, gi, bi):
        ln_stats(t, k)
        ln_affine(t, k)

    def x_dma(j, t):
        r0 = j * T + t * 128
        dma("pool", xst[:, 0, :], I["x"][r0:r0 + 128, :], [], [B("xst0")])

    def x_cast(j, t):
        act(xbt[:, 0, :], xst[:, 0, :], AF.Copy, [B("xst0")], [B("xbt0")])

    FOB = [[4, 5, 6, 7], [0, 1, 2, 3]]

    def x_load(j, t):
        x_dma(j, t)
        x_cast(j, t)

    def x_tr(j, t):
        s2 = 0
        bk = nextbank(0, 2)
        def f(e, s2=s2, bk=bk):
            ins = None
            for kt in range(8):
                ins = e.transpose(out=PSBF(bk)[:, kt * 128:(kt + 1) * 128], in_=xbt[:, s2, kt * 128:(kt + 1) * 128], identity=ident[:])
            return ins
        P.op("pe", f, reads=[B("xbt%d" % s2), B("ident")], writes=[psb[bk]])
        P.op("dve", lambda e, bk=bk, t=t: e.tensor_copy(out=xT[:, :, t * 128:(t + 1) * 128],
                                                       in_=PSBF(bk).rearrange("p (k q) -> p k q", k=8)),
             reads=[psb[bk]], writes=[B("xT")])

    def x_tile(j, t):
        x_load(j, t)
        x_tr(j, t)

    def x_stage(j):
        for t in range(4):
            x_tile(j, t)

    def k_proj(j):
        tok0 = j * T
        wks = [wtake(PW[2]), wtake(PW[3])]
        for hp in range(4):
            wk, wkb = wks[hp // 2]
            bk = FOB[0][nextbank(0, 4)]
            def f(e, hp=hp, bk=bk, wk=wk):
                ins = None
                for kt in range(8):
                    ins = e.matmul(PSB(bk), lhsT=wk[:, kt, (hp % 2) * 128:(hp % 2 + 1) * 128], rhs=xT[:, kt, :], start=(kt == 0), stop=(kt == 7))
                return ins
            P.op("pe", f, reads=[wkb, B("xT")], writes=[psb[bk]])
            for hh in range(2):
                h = 2 * hp + hh
                s2 = h % 4
                act(kTs[:, s2, :], PSB(bk)[hh * 64:(hh + 1) * 64, :], AF.Copy, [psb[bk]], [B("kTs%d" % s2)])
                for a_ in range(2):
                    P.op("dve", lambda e, s2=s2, a_=a_: e.reduce_sum(out=ksum[:, s2, a_:a_ + 1], in_=kTs[:, s2, a_ * 256:(a_ + 1) * 256], axis=AX.X),
                         reads=[B("kTs%d" % s2)] + ([B("ksum%d" % s2)] if a_ else []), writes=[B("ksum%d" % s2)])
                P.op("dve", lambda e, h=h, s2=s2, j=j: e.tensor_scalar(out=kmT[:, h, 2 * j:2 * j + 2], in0=ksum[:, s2, :], scalar1=1.0 / MOBA_BLOCK,
                                                                scalar2=None, op0=ALU.mult),
                     reads=[B("ksum%d" % s2)], writes=[B("kmT")])
                dma("act", ks_d[h, :, tok0:tok0 + T], kTs[:, s2, :], [B("kTs%d" % s2)], [B("ksd%d" % h)])
        issue_load()
        issue_load()

    def q_proj(j):
        wqs = [wtake(PW[0]), wtake(PW[1])]
        for hp in range(4):
            wq, wqb = wqs[hp // 2]
            bk = FOB[1][hp]
            def f(e, hp=hp, bk=bk, wq=wq):
                ins = None
                for kt in range(8):
                    ins = e.matmul(PSB(bk), lhsT=wq[:, kt, (hp % 2) * 128:(hp % 2 + 1) * 128], rhs=xT[:, kt, :], start=(kt == 0), stop=(kt == 7))
                return ins
            P.op("pe", f, reads=[wqb, B("xT")], writes=[psb[bk]])
            act(qaug[0:64, 2 * hp, :], PSB(bk)[0:64, :], AF.Copy, [psb[bk]], [B("qa%d" % (2 * hp))])
            act(qaug[0:64, 2 * hp + 1, :], PSB(bk)[64:128, :], AF.Copy, [psb[bk]], [B("qa%d" % (2 * hp + 1))])
        issue_load()
        issue_load()

    def v_proj(j):
        wvs = [wtake(PW[4]), wtake(PW[5])]
        for t in range(4):
            bk = FOB[0][t]
            def f(e, t=t, bk=bk, wvs=wvs):
                ins = None
                for c_ in range(2):
                    for kt in range(8):
                        ins = e.matmul(PSB(bk)[:, c_ * 256:(c_ + 1) * 256], lhsT=xT[:, kt, t * 128:(t + 1) * 128], rhs=wvs[c_][0][:, kt, :],
                                       start=(kt == 0), stop=(kt == 7))
                return ins
            P.op("pe", f, reads=[wvs[0][1], wvs[1][1], B("xT")], writes=[psb[bk]])
            act(vtmp[:, :, t, 0:64], PSB(bk).rearrange("p (h d) -> p h d", h=8), AF.Copy, [psb[bk]], [B("vtmp")])
        issue_load()
        issue_load()
        dma("act", vs_d[:, :, 4 * j:4 * j + 4, :].rearrange("h p t c -> p h t c"), vtmp[:],
            [B("vtmp")], [B("vsd")])

    x_stage(0)
    try:
      stop_here("setup")
      for j in range(NCH):
          tok0 = j * T
          if j == 0:
              k_proj(j)
          if j == 0:
              q_proj(0)
          if j == 0:
              v_proj(0)
          nk = (j + 1) * 4

          def stage_load(h):
              b_ = h % 2
              dma("act", kst[0:64, b_, 0:nk * 128], ks_d[h, :, 0:nk * 128], [B("ksd%d" % h)], [B("kst%d" % b_)])
              dma("act", vst[:, b_, 0:nk, :], vs_d[h, :, 0:nk, :], [B("vsd")], [B("vst%d" % b_)])
          stage_load(0)
          def fgate(e):
              ins = None
              for h in range(8):
                  for t in range(4):
                      ins = e.matmul(PSB(5)[:, h * 64 + t * 16:h * 64 + (t + 1) * 16], lhsT=qaug[0:64, h, t * 128:(t + 1) * 128], rhs=kmT[:, h, :],
                                     start=True, stop=True)
              return ins
          P.op("pe", fgate, reads=[B("qa%d" % h) for h in range(8)] + [B("kmT")], writes=[psb[5]])
          pmq = pmt[:, 2 * j, :]
          pm_b = bass.AP(tensor=pmq.tensor, offset=pmq.offset, ap=[list(pmq.ap[0]), [16, 2], [0, 2], [1, 16]])
          owq = ownt[:, 2 * j, :]
          own_b = bass.AP(tensor=owq.tensor, offset=owq.offset, ap=[list(owq.ap[0]), [16, 2], [0, 2], [1, 16]])
          def chain(h):
              g2 = h % 2
              P.op("dve", lambda e, h=h, g2=g2, pm_b=pm_b: e.tensor_tensor(out=t1[:, g2, :, :].rearrange("p (a b) n -> p a b n", a=2),
                                                               in0=PSB(5)[:, h * 64:(h + 1) * 64].rearrange("p (a b n) -> p a b n", a=2, b=2),
                                                               in1=pm_b, op=ALU.add),
                   reads=[psb[5], B("pmt")], writes=[B("t1_%d" % g2)])
              for t in range(4):
                  P.op("dve", lambda e, g2=g2, t=t: e.max(out=m8[:, g2, t, :], in_=t1[:, g2, t, :]), reads=[B("t1_%d" % g2)], writes=[B("m8_%d" % g2)])
              P.op("dve", lambda e, g2=g2: e.tensor_scalar(out=thr[:, g2, :], in0=m8[:, g2, :, 2], scalar1=-1e29, scalar2=None, op0=ALU.max),
                   reads=[B("m8_%d" % g2)], writes=[B("thr%d" % g2)])
              for t in range(4):
                  P.op("dve", lambda e, g2=g2, t=t: e.tensor_scalar(out=sel[:, g2, t, :], in0=t1[:, g2, t, :], scalar1=thr[:, g2, t:t + 1], scalar2=None,
                                                                  op0=ALU.is_ge),
                       reads=[B("t1_%d" % g2), B("thr%d" % g2)], writes=[B("sel%d" % g2)])
              P.op("dve", lambda e, g2=g2, own_b=own_b: e.tensor_tensor(out=sel[:, g2, :, :].rearrange("p (a b) n -> p a b n", a=2),
                                                          in0=sel[:, g2, :, :].rearrange("p (a b) n -> p a b n", a=2), in1=own_b, op=ALU.add),
                   reads=[B("sel%d" % g2), B("ownt")], writes=[B("sel%d" % g2)])
              P.op("dve", lambda e, g2=g2, h=h: e.tensor_scalar(out=amb[:, h, :, :], in0=sel[:, g2, :, :], scalar1=-NEG, scalar2=None, op0=ALU.mult),
                   reads=[B("sel%d" % g2)], writes=[B("amb%d" % h)])

          def gate_T(h):
              bk = 6 + (h % 2)
              def f(e, h=h, bk=bk):
                  ins = None
                  for t in range(4):
                      ins = e.transpose(out=PSBF(bk)[0:16, t * 128:(t + 1) * 128], in_=amb[:, h, t, :], identity=ident[:])
                  return ins
              P.op("pe", f, reads=[B("amb%d" % h), B("ident")], writes=[psb[bk]])
              P.op("dve", lambda e, h=h, bk=bk: e.tensor_copy(out=qaug[64:80, h, :], in_=PSBF(bk)[0:16, 0:512]),
                   reads=[psb[bk]], writes=[B("qa%d" % h)])
          chain(0)
          chain(1)
          wps = [wtake(PW[6]), wtake(PW[7])]
          for t in range(4):
              bk = nextbank(0, 4)
              def f(e, t=t, bk=bk, wps=wps):
                  ins = None
                  for c_ in range(2):
                      for kt in range(8):
                          ins = e.matmul(PSB(bk)[:, c_ * 256:(c_ + 1) * 256], lhsT=xT[:, kt, t * 128:(t + 1) * 128], rhs=wps[c_][0][:, kt, :],
                                         start=(kt == 0), stop=(kt == 7))
                  return ins
              P.op("pe", f, reads=[wps[0][1], wps[1][1], B("xT")], writes=[psb[bk]])
              act(ptm[:, 1 + t, :], PSB(bk), AF.Copy, [psb[bk]], [B("ptm%d" % (1 + t))])
          issue_load()
          issue_load()
          gate_T(0)
          gate_T(1)
          for h_ in range(2, 8):
              chain(h_)
          for g in range(4):
              bk = nextbank(0, 4)
              def f(e, g=g, bk=bk, j=j):
                  ins = None
                  for t in range(4):
                      o_ = PSB(bk)[:, t * 128:(t + 1) * 128]
                      if j == 0 and t == 0:
                          e.matmul(o_, lhsT=ptm[:, 1, g * 128:(g + 1) * 128], rhs=mtab[:, 8 + g, :], start=True, stop=False)
                          ins = e.matmul(o_, lhsT=ptm[:, 1, g * 128:(g + 1) * 128], rhs=mtab[:, 12 + g, :], start=False, stop=True)
                      else:
                          e.matmul(o_, lhsT=ptm[:, 1 + t, g * 128:(g + 1) * 128], rhs=mtab[:, 0 + g, :], start=True, stop=False)
                          ins = e.matmul(o_, lhsT=ptm[:, t, g * 128:(g + 1) * 128], rhs=mtab[:, 4 + g, :], start=False, stop=True)
                  return ins
              P.op("pe", f, reads=[B("ptm%d" % i) for i in range(5)] + [B("ptm0"), B("mtab")], writes=[psb[bk]])
              act(diffT[:, g, :], PSB(bk), AF.Copy, [psb[bk]], [B("diffT%d" % g)])
          for g in range(4):
              bk2 = nextbank(0, 4)
              P.op("pe", lambda e, g=g, bk2=bk2: e.matmul(PSB(bk2), lhsT=wpool[:, g, :], rhs=diffT[:, g, :], start=True, stop=True),
                   reads=[B("wpool"), B("diffT%d" % g)], writes=[psb[bk2]])
              act(ypgT[:, g, :], PSB(bk2), AF.Copy, [psb[bk2], B("prm")], [B("ypgT")], scale=psc(g))
          P.op("pool", lambda e: e.tensor_copy(out=ptm[:, 0, :], in_=ptm[:, 4, :]), reads=[B("ptm4")], writes=[B("ptm0")])
          stop_here("gate")
          nkt = 4 * j + 4
          nun = 2 * j + 2
          units = [(h, ui_) for h in range(8) for ui_ in range(nun)]

          def unit_tiles(u):
              h, ui_ = u
              sbk = 2 * (ui_ % 2)
              return [(2 * ui_ + d_, sbk + d_) for d_ in range(2)]

          def emit_S(u):
              h, ui_ = u
              b_ = h % 2
              tiles = unit_tiles(u)
              def f(e, tiles=tiles, h=h, b_=b_, j=j):
                  ins = None
                  for kt, bk in tiles:
                      i_ = kt - 4 * j
                      c0 = 128 * i_ if i_ > 0 else 0
                      ins = e.matmul(PSB(bk)[:, c0:T], lhsT=kst[0:80, b_, kt * 128:(kt + 1) * 128], rhs=qaug[0:80, h, c0:T],
                                     start=True, stop=(i_ < 0))
                      if i_ >= 0:
                          nd = 2 if i_ < 3 else 1
                          for ty in range(nd):
                              cs = c0 + 128 * ty
                              e.matmul(PSB(bk)[:, cs:cs + 128], lhsT=ident[:], rhs=dth[:, 2 * h + ty, :], start=False, stop=False)
                              ins = e.matmul(PSB(bk)[:, cs:cs + 128], lhsT=ident[:], rhs=dtl[:, 2 * h + ty, :], start=False,
                                             stop=(ty == nd - 1))
                  return ins
              P.op("pe", f, reads=[B("kst%d" % b_), B("qa%d" % h), B("dtab"), B("ident")], writes=[psb[bk] for _, bk in tiles])

          def emit_exp(u):
              tiles = unit_tiles(u)
              sbk = tiles[0][1]
              act(pT[:, sbk:sbk + 2, :], psum[:, sbk:sbk + 2, :], AF.Exp, [psb[sbk], psb[sbk + 1]],
                  [B("pT%d" % sbk), B("pT%d" % (sbk + 1))], scale=0.125)

          def emit_PV(u):
              h, ui_ = u
              b_ = h % 2
              ob = 4 + h % 2
              tiles = unit_tiles(u)
              def f(e, tiles=tiles, b_=b_, ob=ob, j=j, nkt=nkt):
                  ins = None
                  for kt, bk in tiles:
                      i_ = kt - 4 * j
                      c0 = 128 * i_ if i_ > 0 else 0
                      ins = e.matmul(PSB(ob)[0:65, c0:T], lhsT=vst[:, b_, kt, :], rhs=pT[:, bk, c0:T], start=(kt == 0), stop=(kt == nkt - 1))
                  return ins
              P.op("pe", f, reads=[B("vst%d" % b_)] + [B("pT%d" % bk) for _, bk in tiles], writes=[psb[ob]])

          def finish_a(h):
              ob = 4 + h % 2
              hb2 = h % 2
              rec_ = lr_[:, (2 + hb2) * T:(3 + hb2) * T]
              P.op("dve", lambda e, rec_=rec_, ob=ob: e.reciprocal(out=rec_, in_=PSB(ob)[64:65, :]), reads=[psb[ob]], writes=[B("rec%d" % hb2)])

          def finish_b(h):
              ob = 4 + h % 2
              hb2 = h % 2
              rec_ = lr_[:, (2 + hb2) * T:(3 + hb2) * T]
              bb = 6 + hb2
              hl_ = bcs_full[64:65, hb2, :].bitcast(BF16)
              P.op("dve", lambda e, rec_=rec_, hl_=hl_: e.tensor_copy(out=hl_[:, 0:T], in_=rec_), reads=[B("rec%d" % hb2)], writes=[B("hl%d" % hb2)])
              P.op("dve", lambda e, rec_=rec_, hl_=hl_: e.tensor_tensor(out=hl_[:, T:2 * T], in0=rec_, in1=hl_[:, 0:T], op=ALU.subtract),
                   reads=[B("rec%d" % hb2), B("hl%d" % hb2)], writes=[B("hl%d" % hb2)])
              def fbc(e, hl_=hl_, bb=bb):
                  e.matmul(PSB(bb)[0:64, :], lhsT=onesb[64:65, :], rhs=hl_[:, 0:T], start=True, stop=False)
                  return e.matmul(PSB(bb)[0:64, :], lhsT=onesb[64:65, :], rhs=hl_[:, T:2 * T], start=False, stop=True)
              P.op("pe", fbc, reads=[B("onesb"), B("hl%d" % hb2)], writes=[psb[bb]])
              P.op("dve", lambda e, hb2=hb2, bb=bb: e.tensor_copy(out=bcs[:, hb2, :], in_=PSB(bb)[0:64, :]), reads=[psb[bb]], writes=[B("bcs%d" % hb2)])
              hp_ = h // 2
              if h % 2 == 0:
                  P.op("dve", lambda e, hp_=hp_, hb2=hb2, ob=ob: e.tensor_tensor(out=attnT[:, hp_, :], in0=PSB(ob)[0:64, :], in1=bcs[:, hb2, :], op=ALU.mult),
                       reads=[psb[ob], B("bcs%d" % hb2)], writes=[B("attnT%d" % h)])
              else:
                  P.op("dve", lambda e, hp_=hp_, hb2=hb2, ob=ob: e.tensor_tensor(out=attnT[:, 4 + hp_, :], in0=PSB(ob)[0:64, :], in1=bcs[:, hb2, :], op=ALU.mult),
                       reads=[psb[ob], B("bcs%d" % hb2)], writes=[B("attnTo%d" % h)])
                  P.op("pool", lambda e, hp_=hp_: e.tensor_copy(out=attnT_full[64:128, hp_, :], in_=attnT_full[0:64, 4 + hp_, :]),
                       reads=[B("attnTo%d" % h)], writes=[B("attnT%d" % h)])

          emit_S(units[0])
          if len(units) > 1:
              emit_S(units[1])
          pend = []
          DELAY = min(5, nun - 1)
          for ui, u in enumerate(units):
              h = u[0]
              first_of_head = (ui % nun == 0)
              last_of_head = (ui % nun == nun - 1)
              if first_of_head and 1 <= h and h + 1 < 8:
                  gate_T(h + 1)
              if (ui % nun == min(2, nun - 2)) and h + 1 < 8:
                  stage_load(h + 1)
              emit_exp(u)
              if ui + 2 < len(units):
                  emit_S(units[ui + 2])
              emit_PV(u)
              for pe_ in pend:
                  pe_[1] -= 1
              while pend and pend[0][1] <= 0:
                  finish_b(pend.pop(0)[0])
              if last_of_head:
                  finish_a(h)
                  pend.append([h, DELAY])
          while pend:
              finish_b(pend.pop(0)[0])
          stop_here("attn")
          for dp in range(4):
              wg0, wg0b = wtake(PW[8 + dp])
              wg1, wg1b = wtake(PW[12 + dp])
              wba, wbab = wtake(PBA[dp])
              wbp, wbpb = wtake(PBP[dp])
              for ci in range(2):
                  i_ = dp * 2 + ci
                  k4 = i_ % 2
                  bya, byp, bg0, bg1 = [4 * k4 + x_ for x_ in range(4)]
                  def fyp(e, ci=ci, b=byp, wbp=wbp):
                      ins = None
                      for g in range(4):
                          ins = e.matmul(PSB(b), lhsT=wbp[:, g, ci * 128:(ci + 1) * 128], rhs=ypgT[:, g, :], start=(g == 0), stop=(g == 3))
                      return ins
                  P.op("pe", fyp, reads=[wbpb, B("ypgT")], writes=[psb[byp]])
                  for (wg, wgb, b, si) in ((wg0, wg0b, bg0, 0), (wg1, wg1b, bg1, 1)):
                      def fg(e, ci=ci, b=b, wg=wg):
                          ins = None
                          for kt in range(8):
                              ins = e.matmul(PSB(b), lhsT=wg[:, kt, ci * 128:(ci + 1) * 128], rhs=xT[:, kt, :], start=(kt == 0), stop=(kt == 7))
                          return ins
                      P.op("pe", fg, reads=[wgb, B("xT")], writes=[psb[b]])
                      act(sg[:, si, :], PSB(b), AF.Sigmoid, [psb[b]], [B("sg%d" % si)])
                  def fya(e, ci=ci, b=bya, wba=wba):
                      ins = None
                      for hp_ in range(4):
                          ins = e.matmul(PSB(b), lhsT=wba[:, hp_, ci * 128:(ci + 1) * 128], rhs=attnT_full[:, hp_, :], start=(hp_ == 0), stop=(hp_ == 3))
                      return ins
                  P.op("pe", fya, reads=[wbab] + [B("attnT%d" % h) for h in range(8)], writes=[psb[bya]])
                  P.op("dve", lambda e, b=byp: e.tensor_tensor(out=mm[:, 1, :], in0=sg[:, 1, :], in1=PSB(b), op=ALU.mult),
                       reads=[psb[byp], B("sg1")], writes=[B("mm1")])
                  P.op("dve", lambda e, b=bya: e.tensor_tensor(out=mm[:, 0, :], in0=sg[:, 0, :], in1=PSB(b), op=ALU.mult),
                       reads=[psb[bya], B("sg0")], writes=[B("mm0")])
                  P.op("pool", lambda e, i_=i_: e.tensor_tensor(out=mixedT[:, i_, :], in0=mm[:, 0, :], in1=mm[:, 1, :], op=ALU.add),
                       reads=[B("mm0"), B("mm1")], writes=[B("mixedT")] + QA)
              for _ in range(4):
                  issue_load()
          stop_here("merge")
          wos = [wtake(PWO[q_]) for q_ in range(4)]
          load_ln(1)
          xrb = [xst[:, 0, :], stmp[:].rearrange("p a b -> p (a b)")]
          xrbn = ["xst0", "xst1"]
          h1bb = [h1b[:, 0, :], mm[:, 0, :].bitcast(BF16)]
          h1bn = ["h1b0", "mm0"]

          def ln1_mm(t):
              for hf in range(2):
                  bk = 2 * t + hf
                  def f(e, t=t, bk=bk, hf=hf, wos=wos):
                      ins = None
                      for c_ in range(2):
                          wo = wos[2 * hf + c_][0]
                          for kt in range(8):
                              ins = e.matmul(PSB(bk)[:, c_ * 256:(c_ + 1) * 256], lhsT=mixedT[:, kt, t * 128:(t + 1) * 128], rhs=wo[:, kt, :],
                                             start=(kt == 0), stop=(kt == 7))
                      return ins
                  P.op("pe", f, reads=[wos[2 * hf][1], wos[2 * hf + 1][1], B("mixedT")] + QA, writes=[psb[bk]])

          def ln1_A(t):
              s2 = t % 2
              r0 = tok0 + t * 128
              dma("pool", xrb[s2], I["x"][r0:r0 + 128, :], [], [B(xrbn[s2])])
              hb = B("h1_%d" % t)
              for hf in range(2):
                  bk = 2 * t + hf
                  P.op("dve", lambda e, t=t, hf=hf, s2=s2, bk=bk: e.scalar_tensor_tensor(
                      out=h1[:, t, hf * 512:(hf + 1) * 512], in0=xrb[s2][:, hf * 512:(hf + 1) * 512], scalar=ALPHA, in1=PSB(bk),
                      op0=ALU.mult, op1=ALU.add), reads=[psb[bk], B(xrbn[s2]), hb], writes=[hb])
              ln_stats(t, t % 2)

          def ln1_B(t):
              s2 = t % 2
              hb = B("h1_%d" % t)
              ln_affine(t, t % 2)
              act(h1bb[s2], h1[:, t, :], AF.Copy, [hb], [B(h1bn[s2])])
              bk = 2 * t
              def f(e, s2=s2, bk=bk):
                  ins = None
                  for kt in range(8):
                      ins = e.transpose(out=PSBF(bk)[:, kt * 128:(kt + 1) * 128], in_=h1bb[s2][:, kt * 128:(kt + 1) * 128], identity=ident[:])
                  return ins
              P.op("pe", f, reads=[B(h1bn[s2]), B("ident")], writes=[psb[bk]])

          def ln1_C(t):
              bk = 2 * t
              act(h1T[:, :, t * 128:(t + 1) * 128], PSBF(bk).rearrange("p (k q) -> p k q", k=8), AF.Copy, [psb[bk]], [B("xT")])

          for t in range(4):
              ln1_mm(t)
          for step in range(6):
              if step < 4:
                  ln1_A(step)
              if 0 <= step - 1 < 4:
                  ln1_B(step - 1)
              if 0 <= step - 2 < 4:
                  ln1_C(step - 2)
          for _ in range(4):
              issue_load()
          stop_here("ln1")
          if j + 1 < NCH:
              x_load(j + 1, 0)
              x_dma(j + 1, 1)
          for m_ in range(11):
              wa, wab = wtake(PA[m_])
              wu, wub = wtake(PU[m_])
              for ci in range(2):
                  f_ = m_ * 2 + ci
                  s2 = 0
                  ba = (2 * f_) % 4
                  bu = (2 * f_ + 1) % 4
                  for (w_, wb_, b) in ((wa, wab, ba), (wu, wub, bu)):
                      def fm(e, ci=ci, b=b, w_=w_):
                          ins = None
                          for kt in range(8):
                              ins = e.matmul(PSB(b), lhsT=w_[:, kt, ci * 128:(ci + 1) * 128], rhs=h1T[:, kt, :], start=(kt == 0), stop=(kt == 7))
                          return ins
                      P.op("pe", fm, reads=[wb_, B("xT")], writes=[psb[b]])
                  ab = B("asb%d" % s2)
                  P.op("pool", lambda e, f_=f_, s2=s2: e.tensor_copy(out=asb[:, s2, 0:2], in_=halo[:, f_, :]), reads=[B("halo")], writes=[ab])
                  act(asb[:, s2, 2:T + 2], PSB(ba), AF.Copy, [psb[ba], ab], [ab])
                  P.op("pool", lambda e, f_=f_, s2=s2: e.tensor_copy(out=halo[:, f_, :], in_=asb[:, s2, T:T + 2]), reads=[ab], writes=[B("halo")])
                  P.op("dve", lambda e, f_=f_, s2=s2: e.tensor_scalar(out=o1[:, s2, :], in0=asb[:, s2, 2:T + 2], scalar1=cw(f_, 2), scalar2=None, op0=ALU.mult),
                       reads=[ab, B("prm")], writes=[B("o1_%d" % s2)])
                  P.op("dve", lambda e, f_=f_, s2=s2: e.scalar_tensor_tensor(out=o2[:, s2, :], in0=asb[:, s2, 1:T + 1], scalar=cw(f_, 1), in1=o1[:, s2, :],
                                                                            op0=ALU.mult, op1=ALU.add),
                       reads=[ab, B("prm"), B("o1_%d" % s2)], writes=[B("o2_%d" % s2)])
                  P.op("dve", lambda e, f_=f_, s2=s2: e.scalar_tensor_tensor(out=o1[:, s2, :], in0=asb[:, s2, 0:T], scalar=cw(f_, 0), in1=o2[:, s2, :],
                                                                            op0=ALU.mult, op1=ALU.add),
                       reads=[ab, B("prm"), B("o2_%d" % s2), B("o1_%d" % s2)], writes=[B("o1_%d" % s2)])
                  act(gl[:, s2, :], o1[:, s2, :], AF.Gelu, [B("o1_%d" % s2), B("prm")], [B("gl%d" % s2)], bias=cbv(f_), scale=1.0)
                  P.op("dve", lambda e, f_=f_, s2=s2, bu=bu: e.tensor_tensor(out=guT[:, f_, :], in0=gl[:, s2, :], in1=PSB(bu), op=ALU.mult),
                       reads=[B("gl%d" % s2), psb[bu]], writes=[B("guT%d" % f_)])
              issue_load()
              issue_load()
          stop_here("ffnin")
          load_ln(2)
          for tp_ in range(2):
              accs = {}
              for m_ in range(11):
                  wf, wfb = wtake(PFO[m_])
                  nk_ = 2
                  def fo(e, m_=m_, nk_=nk_, wf=wf, tp_=tp_):
                      ins = None
                      for kl in range(nk_):
                          kt = m_ * 2 + kl
                          for tt_ in range(2):
                              t = 2 * tp_ + tt_
                              for hf in range(2):
                                  b = FOB[tp_][2 * tt_ + hf]
                                  ins = e.matmul(PSB(b), lhsT=guT[:, kt, t * 128:(t + 1) * 128], rhs=wf[:, kl, hf * 512:(hf + 1) * 512],
                                                 start=(kt == 0), stop=(kt == NFF - 1))
                      return ins
                  P.op("pe", fo, reads=[wfb, B("guT%d" % (2 * m_)), B("guT%d" % (2 * m_ + 1))], writes=[psb[FOB[tp_][x_]] for x_ in range(4)])
                  issue_load()
                  if tp_ == 0 and m_ % 2 == 1 and 3 <= m_ < 10 and j + 1 < NCH:
                      xt_ = (m_ - 3) // 2
                      x_tr(j + 1, xt_)
                      if xt_ + 1 < 4:
                          x_cast(j + 1, xt_ + 1)
                      if xt_ + 2 < 4:
                          x_dma(j + 1, xt_ + 2)
              for tt_ in range(2):
                  t = 2 * tp_ + tt_
                  hb = B("h1_%d" % t)
                  for hf in range(2):
                      b = FOB[tp_][2 * tt_ + hf]
                      P.op("dve", lambda e, t=t, hf=hf, b=b: e.scalar_tensor_tensor(
                          out=h1[:, t, hf * 512:(hf + 1) * 512], in0=h1[:, t, hf * 512:(hf + 1) * 512], scalar=ALPHA, in1=PSB(b),
                          op0=ALU.mult, op1=ALU.add), reads=[psb[b], hb], writes=[hb])
              if tp_ == 1 and j + 1 < NCH:
                  k_proj(j + 1)
                  q_proj(j + 1)
                  v_proj(j + 1)
              for tt_ in range(2):
                  t = 2 * tp_ + tt_
                  hb = B("h1_%d" % t)
                  layer_norm(t, t % 2, 2, 3)
                  r0 = tok0 + t * 128
                  dma("pool", out[r0:r0 + 128, :], h1[:, t, :], [hb], [B("outd%d" % t)])
    except _Stop:
        pass
    P.wait_all("pool", [B("outd%d" % t) for t in range(4)])
    P.emit(nc)
    st.close()
    return nc


_CACHE = {}


def kernel(x, w_in, rel_bias, w_pool_group, pool_scale, w_branch_attn, w_branch_pool, w_out, ln1_g, ln1_b,
           w_ffn_in, conv_w, conv_b, w_ffn_out, ln2_g, ln2_b):
    f32 = lambda a: np.ascontiguousarray(np.asarray(a, dtype=np.float32))
    x = f32(x)
    nb, seq, _ = x.shape
    if "nc" not in _CACHE:
        _CACHE["nc"] = build(seq)
        _CACHE["consts"] = make_consts()
    nc = _CACHE["nc"]
    shared = {
        "w_in": f32(w_in)[0], "rel_bias": f32(rel_bias), "w_pool_group": f32(w_pool_group)[0],
        "pool_scale": f32(pool_scale)[0].reshape(4, 128), "w_branch_attn": f32(w_branch_attn)[0],
        "w_branch_pool": f32(w_branch_pool)[0], "w_out": f32(w_out)[0], "ln1_g": f32(ln1_g)[0].reshape(1, 1024),
        "ln1_b": f32(ln1_b)[0].reshape(1, 1024), "w_ffn_in": f32(w_ffn_in)[0],
        "conv_w": f32(conv_w)[0].reshape(3 * NFF, 128), "conv_b": f32(conv_b)[0].reshape(NFF, 128),
        "w_ffn_out": f32(w_ffn_out)[0], "ln2_g": f32(ln2_g)[0].reshape(1, 1024), "ln2_b": f32(ln2_b)[0].reshape(1, 1024),
    }
    shared.update(_CACHE["consts"])
    in_maps = [dict(shared, x=x[b]) for b in range(nb)]
    res = run_bass_kernel_spmd(nc, in_maps, core_ids=list(range(nb)))
    return np.stack([np.asarray(r["out"], dtype=np.float32) for r in res.results], axis=0)
```
